# Optimizing a Trainium2 kernel written in Bass

```python
import math
import jax
import jax.numpy as jnp
from jax import lax
import numpy as np


D_MODEL = 1024
BATCH = 8
SEQ = 8192
DEPTH = 4
DEC_BATCH = 2
DEC_SEQ = 8192
PAST_LEN = 128

GRID_W = 64
Q_BLOCK = 128
ATT_Q_HEADS = 8
ATT_KV_HEADS = 2
ATT_HEAD_DIM = 64
ATT_GROUP = ATT_Q_HEADS // ATT_KV_HEADS
ROPE_PAIRS = ATT_HEAD_DIM // 4
ROPE_THETA = 10000.0
NA_HEADS = 8
NA_HEAD_DIM = 64
NA_WIN_ROWS = 8
NA_WIN_COLS = 16
DN_HEADS = 8
DN_HEAD_DIM = 64
DN_CHUNK = 64
DN_CONV_W = 3
BRANCH_WIDTH = 512
N_BRANCHES = 3
FFN_HIDDEN = ((8 * D_MODEL + 3 * 256 - 1) // (3 * 256)) * 256
PLE_DIM = 256
LN_EPS = 1e-5
RMS_EPS = 1e-6
L2_EPS = 1e-6
DEEPNORM_ALPHA = (2 * DEPTH) ** 0.25
DEEPNORM_BETA = (8 * DEPTH) ** -0.25
IN_SIZES = (ATT_Q_HEADS * ATT_HEAD_DIM, ATT_KV_HEADS * ATT_HEAD_DIM, ATT_KV_HEADS * ATT_HEAD_DIM,
            NA_HEADS * NA_HEAD_DIM, NA_HEADS * NA_HEAD_DIM, NA_HEADS * NA_HEAD_DIM,
            DN_HEADS * DN_HEAD_DIM, DN_HEADS * DN_HEAD_DIM, DN_HEADS * DN_HEAD_DIM, DN_HEADS * DN_HEAD_DIM,
            DN_HEADS, DN_HEADS, DN_HEADS, DN_HEADS)
IN_COLS = sum(IN_SIZES)

kernel_name = 'hybrid_grid_encoder'


def layer_norm(x, g, b):
    xf = x.astype(jnp.float32)
    mu = jnp.mean(xf, axis=-1, keepdims=True)
    var = jnp.mean(jnp.square(xf - mu), axis=-1, keepdims=True)
    y = (xf - mu) * lax.rsqrt(var + LN_EPS) * g.astype(jnp.float32) + b.astype(jnp.float32)
    return y.astype(x.dtype)


def rms_norm(x, g):
    xf = x.astype(jnp.float32)
    y = xf * lax.rsqrt(jnp.mean(jnp.square(xf), axis=-1, keepdims=True) + RMS_EPS) * g.astype(jnp.float32)
    return y.astype(x.dtype)


def l2_normalize(x):
    xf = x.astype(jnp.float32)
    return (xf * lax.rsqrt(jnp.sum(jnp.square(xf), axis=-1, keepdims=True) + L2_EPS)).astype(x.dtype)


def axial_rope_tables(seq_len):
    t = jnp.arange(seq_len, dtype=jnp.int32)
    row = (t // GRID_W).astype(jnp.float32)
    col = (t % GRID_W).astype(jnp.float32)
    inv_freq = ROPE_THETA ** (-jnp.arange(ROPE_PAIRS, dtype=jnp.float32) / ROPE_PAIRS)
    ang_r = row[:, None] * inv_freq[None, :]
    ang_c = col[:, None] * inv_freq[None, :]
    return (jnp.cos(ang_r), jnp.sin(ang_r), jnp.cos(ang_c), jnp.sin(ang_c))


def apply_axial_rope(x, tables):
    cr, sr, cc, sc = (a[None, :, None, :] for a in tables)
    xf = x.astype(jnp.float32)
    P = ROPE_PAIRS
    r1, r2, c1, c2 = xf[..., :P], xf[..., P:2 * P], xf[..., 2 * P:3 * P], xf[..., 3 * P:]
    out = jnp.concatenate([r1 * cr - r2 * sr, r2 * cr + r1 * sr,
                           c1 * cc - c2 * sc, c2 * cc + c1 * sc], axis=-1)
    return out.astype(x.dtype)


def gqa_attention(q, k, v):
    b, s, _, hd = q.shape
    nblk = s // Q_BLOCK
    qb = jnp.moveaxis(q.reshape(b, nblk, Q_BLOCK, ATT_KV_HEADS, ATT_GROUP, hd), 1, 0)
    scale = hd ** -0.5

    def block(qi):
        sc = jnp.einsum('bqhgd,bkhd->bhgqk', qi, k, preferred_element_type=jnp.float32) * scale
        pr = jax.nn.softmax(sc, axis=-1)
        return jnp.einsum('bhgqk,bkhd->bqhgd', pr.astype(v.dtype), v)

    o = lax.map(block, qb)
    return jnp.moveaxis(o, 0, 1).reshape(b, s, ATT_Q_HEADS * hd)


def neighborhood_attention(q, k, v, rpb):
    b, s, h, hd = q.shape
    rows = s // GRID_W
    wr = min(NA_WIN_ROWS, rows)
    wc = NA_WIN_COLS
    t = jnp.arange(s, dtype=jnp.int32)
    r, c = t // GRID_W, t % GRID_W
    r0 = jnp.clip(r - wr // 2, 0, rows - wr)
    c0 = jnp.clip(c - wc // 2, 0, GRID_W - wc)
    j = jnp.arange(wr * wc, dtype=jnp.int32)
    kr = r0[:, None] + j[None, :] // wc
    kc = c0[:, None] + j[None, :] % wc
    idx = kr * GRID_W + kc
    dr = kr - r[:, None] + (NA_WIN_ROWS - 1)
    dc = kc - c[:, None] + (NA_WIN_COLS - 1)
    nblk = s // Q_BLOCK

    def blk(a):
        return a.reshape((nblk, Q_BLOCK) + a.shape[1:])

    qb = jnp.moveaxis(q.reshape(b, nblk, Q_BLOCK, h, hd), 1, 0)
    scale = hd ** -0.5
    bias_tab = rpb.astype(jnp.float32)

    def block(args):
        qi, ii, dri, dci = args
        ki = k[:, ii]
        vi = v[:, ii]
        sc = jnp.einsum('bqhd,bqwhd->bhqw', qi, ki, preferred_element_type=jnp.float32) * scale
        sc = sc + bias_tab[:, dri, dci][None]
        pr = jax.nn.softmax(sc, axis=-1)
        return jnp.einsum('bhqw,bqwhd->bqhd', pr.astype(vi.dtype), vi)

    o = lax.map(block, (qb, blk(idx), blk(dr), blk(dc)))
    return jnp.moveaxis(o, 0, 1).reshape(b, s, h * hd)


def centred_depthwise_conv(x, w):
    ch = x.shape[-1]
    return lax.conv_general_dilated(x, w[:, None, :].astype(x.dtype), window_strides=(1,),
                                    padding=[(DN_CONV_W // 2, DN_CONV_W // 2)],
                                    dimension_numbers=('NWC', 'WIO', 'NWC'),
                                    feature_group_count=ch)


def gated_delta_rule(q, k, v, g, beta):
    b, s, h, dk = q.shape
    dv = v.shape[-1]
    C = DN_CHUNK
    n = s // C

    def to_chunks(a):
        a = a.astype(jnp.float32).reshape((b, n, C, h) + a.shape[3:])
        return jnp.moveaxis(a, 3, 1)

    q, k, v, g, beta = (to_chunks(a) for a in (q, k, v, g, beta))
    gc = jnp.cumsum(g, axis=-1)
    incl = jnp.tril(jnp.ones((C, C), dtype=bool))
    strict = jnp.tril(jnp.ones((C, C), dtype=bool), -1)
    decay = jnp.exp(jnp.where(incl, gc[..., :, None] - gc[..., None, :], -jnp.inf))
    kb = k * beta[..., None]
    a_mat = jnp.where(strict, jnp.einsum('bhnid,bhnjd->bhnij', kb, k) * decay, 0.0)
    eye = jnp.eye(C, dtype=jnp.float32)
    t_mat = lax.linalg.triangular_solve(eye + a_mat, jnp.broadcast_to(eye, a_mat.shape),
                                        left_side=True, lower=True, unit_diagonal=True)
    u = jnp.einsum('bhnij,bhnjd->bhnid', t_mat, v * beta[..., None])
    w = jnp.einsum('bhnij,bhnjd->bhnid', t_mat, kb * jnp.exp(gc)[..., None])
    intra = jnp.where(incl, jnp.einsum('bhnid,bhnjd->bhnij', q, k) * decay, 0.0)
    q_dec = q * jnp.exp(gc)[..., None]
    k_dec = k * jnp.exp(gc[..., -1:] - gc)[..., None]
    g_tot = jnp.exp(gc[..., -1])

    def step(state, xs):
        u_i, w_i, intra_i, q_i, k_i, gt_i = xs
        v_new = u_i - jnp.einsum('bhcd,bhde->bhce', w_i, state)
        o_i = jnp.einsum('bhcd,bhde->bhce', q_i, state) + jnp.einsum('bhcj,bhje->bhce', intra_i, v_new)
        state = state * gt_i[..., None, None] + jnp.einsum('bhcd,bhce->bhde', k_i, v_new)
        return state, o_i

    xs = tuple(jnp.moveaxis(a, 2, 0) for a in (u, w, intra, q_dec, k_dec, g_tot))
    state0 = jnp.zeros((b, h, dk, dv), jnp.float32)
    _, o = lax.scan(step, state0, xs)
    return jnp.transpose(o, (1, 0, 3, 2, 4)).reshape(b, s, h, dv)


def deltanet_branch(cq, ck, cv, cz, a_f, a_b, b_f, b_b, conv_w, a_log, dt_bias, norm_g):
    bsz, s, _ = cq.shape
    hshape = (bsz, s, DN_HEADS, DN_HEAD_DIM)
    qkv = jax.nn.silu(centred_depthwise_conv(jnp.concatenate([cq, ck, cv], axis=-1), conv_w))
    q, k, v = jnp.split(qkv, 3, axis=-1)
    q = l2_normalize(q.reshape(hshape)) * (DN_HEAD_DIM ** -0.5)
    k = l2_normalize(k.reshape(hshape))
    v = v.reshape(hshape)

    def decay_and_beta(a, bb, d):
        g = -jnp.exp(a_log[d].astype(jnp.float32)) * jax.nn.softplus(a.astype(jnp.float32) + dt_bias[d].astype(jnp.float32))
        return g, jax.nn.sigmoid(bb.astype(jnp.float32))

    g_f, beta_f = decay_and_beta(a_f, b_f, 0)
    g_b, beta_b = decay_and_beta(a_b, b_b, 1)
    o_fwd = gated_delta_rule(q, k, v, g_f, beta_f)

    def rev(a):
        return jnp.flip(a, axis=1)

    o_bwd = rev(gated_delta_rule(rev(q), rev(k), rev(v), rev(g_b), rev(beta_b)))
    o = rms_norm(o_fwd + o_bwd, norm_g) * jax.nn.silu(cz.reshape(hshape).astype(jnp.float32))
    return o.reshape(bsz, s, DN_HEADS * DN_HEAD_DIM).astype(cq.dtype)


def encoder_layer(x, p_i, i, prm):
    bsz, s, _ = x.shape
    split_points = np.cumsum(IN_SIZES)[:-1].tolist()
    (aq, ak, av, nq, nk, nv, cq, ck, cv, cz,
     ca_f, ca_b, cb_f, cb_b) = jnp.split(x @ prm['w_in'][i], split_points, axis=-1)
    tables = axial_rope_tables(s)
    qa = apply_axial_rope(rms_norm(aq.reshape(bsz, s, ATT_Q_HEADS, ATT_HEAD_DIM), prm['att_q_norm_g'][i]), tables)
    ka = apply_axial_rope(rms_norm(ak.reshape(bsz, s, ATT_KV_HEADS, ATT_HEAD_DIM), prm['att_k_norm_g'][i]), tables)
    va = av.reshape(bsz, s, ATT_KV_HEADS, ATT_HEAD_DIM)
    br_a = gqa_attention(qa, ka, va)
    na_shape = (bsz, s, NA_HEADS, NA_HEAD_DIM)
    br_b = neighborhood_attention(nq.reshape(na_shape), nk.reshape(na_shape), nv.reshape(na_shape), prm['na_rpb'][i])
    br_c = deltanet_branch(cq, ck, cv, cz, ca_f, ca_b, cb_f, cb_b, prm['dn_conv_w'][i],
                           prm['dn_a_log'][i], prm['dn_dt_bias'][i], prm['dn_norm_g'][i])
    gates = jax.nn.sigmoid(x @ prm['w_gate'][i] + prm['b_gate'][i])
    g_a, g_b, g_c = jnp.split(gates, N_BRANCHES, axis=-1)
    w_br = prm['w_branch'][i]
    merged = g_a * (br_a @ w_br[0]) + g_b * (br_b @ w_br[1]) + g_c * (br_c @ w_br[2])
    x = layer_norm(DEEPNORM_ALPHA * x + merged @ prm['w_out'][i], prm['ln1_g'][i], prm['ln1_b'][i])
    gate, up = jnp.split(x @ prm['w_ffn_in'][i], 2, axis=-1)
    ffn = (jax.nn.silu(gate) * up) @ prm['w_ffn_out'][i]
    ple = jax.nn.sigmoid(x @ prm['w_ple_gate'][i] + prm['b_ple_gate'][i]) * (p_i @ prm['w_ple_proj'][i])
    return layer_norm(DEEPNORM_ALPHA * x + ffn + ple, prm['ln2_g'][i], prm['ln2_b'][i])


def run_trunk(x, p, emb_ln_g, emb_ln_b, prm):
    x = layer_norm(x, emb_ln_g, emb_ln_b)
    for i in range(DEPTH):
        x = encoder_layer(x, p[i], i, prm)
    return x


def setup_inputs(seed: int = 0) -> dict:
    key = jax.random.key(seed)
    ks = jax.random.split(key, 32)
    f32 = jnp.float32

    def nrm(k, shape, scale):
        return jax.random.normal(k, shape, f32) * scale

    inv_d = D_MODEL ** -0.5
    dt = jnp.exp(jax.random.uniform(ks[12], (DEPTH, 2, DN_HEADS), f32, math.log(1e-3), math.log(1e-1)))
    return {
        'x_prompt': nrm(ks[0], (BATCH, SEQ, D_MODEL), 1.0),
        'x_sample': nrm(ks[1], (DEC_BATCH, DEC_SEQ, D_MODEL), 1.0),
        'p_prompt': nrm(ks[2], (DEPTH, BATCH, SEQ, PLE_DIM), 1.0),
        'p_sample': nrm(ks[3], (DEPTH, DEC_BATCH, DEC_SEQ, PLE_DIM), 1.0),
        'emb_ln_g': 1.0 + nrm(ks[4], (D_MODEL,), 0.02),
        'emb_ln_b': nrm(ks[5], (D_MODEL,), 0.02),
        'w_in': nrm(ks[6], (DEPTH, D_MODEL, IN_COLS), inv_d),
        'att_q_norm_g': 1.0 + nrm(ks[7], (DEPTH, ATT_HEAD_DIM), 0.02),
        'att_k_norm_g': 1.0 + nrm(ks[8], (DEPTH, ATT_HEAD_DIM), 0.02),
        'na_rpb': nrm(ks[9], (DEPTH, NA_HEADS, 2 * NA_WIN_ROWS - 1, 2 * NA_WIN_COLS - 1), 0.02),
        'dn_conv_w': nrm(ks[10], (DEPTH, DN_CONV_W, 3 * DN_HEADS * DN_HEAD_DIM), DN_CONV_W ** -0.5),
        'dn_a_log': jnp.log(jax.random.uniform(ks[11], (DEPTH, 2, DN_HEADS), f32, 1.0, 16.0)),
        'dn_dt_bias': dt + jnp.log(-jnp.expm1(-dt)),
        'dn_norm_g': 1.0 + nrm(ks[13], (DEPTH, DN_HEAD_DIM), 0.02),
        'w_gate': nrm(ks[14], (DEPTH, D_MODEL, N_BRANCHES * D_MODEL), inv_d),
        'b_gate': nrm(ks[15], (DEPTH, N_BRANCHES * D_MODEL), 0.02),
        'w_branch': nrm(ks[16], (DEPTH, N_BRANCHES, BRANCH_WIDTH, D_MODEL), BRANCH_WIDTH ** -0.5),
        'w_out': nrm(ks[17], (DEPTH, D_MODEL, D_MODEL), inv_d * DEEPNORM_BETA),
        'ln1_g': 1.0 + nrm(ks[18], (DEPTH, D_MODEL), 0.02),
        'ln1_b': nrm(ks[19], (DEPTH, D_MODEL), 0.02),
        'w_ffn_in': nrm(ks[20], (DEPTH, D_MODEL, 2 * FFN_HIDDEN), inv_d),
        'w_ffn_out': nrm(ks[21], (DEPTH, FFN_HIDDEN, D_MODEL), FFN_HIDDEN ** -0.5 * DEEPNORM_BETA),
        'w_ple_gate': nrm(ks[22], (DEPTH, D_MODEL, D_MODEL), inv_d),
        'b_ple_gate': nrm(ks[23], (DEPTH, D_MODEL), 0.02),
        'w_ple_proj': nrm(ks[24], (DEPTH, PLE_DIM, D_MODEL), PLE_DIM ** -0.5),
        'ln2_g': 1.0 + nrm(ks[25], (DEPTH, D_MODEL), 0.02),
        'ln2_b': nrm(ks[26], (DEPTH, D_MODEL), 0.02),
    }


def reference(x_prompt, x_sample, p_prompt, p_sample, emb_ln_g, emb_ln_b, w_in,
              att_q_norm_g, att_k_norm_g, na_rpb, dn_conv_w, dn_a_log, dn_dt_bias, dn_norm_g,
              w_gate, b_gate, w_branch, w_out, ln1_g, ln1_b, w_ffn_in, w_ffn_out,
              w_ple_gate, b_ple_gate, w_ple_proj, ln2_g, ln2_b):
    prm = {
        'w_in': w_in, 'att_q_norm_g': att_q_norm_g, 'att_k_norm_g': att_k_norm_g,
        'na_rpb': na_rpb, 'dn_conv_w': dn_conv_w, 'dn_a_log': dn_a_log,
        'dn_dt_bias': dn_dt_bias, 'dn_norm_g': dn_norm_g, 'w_gate': w_gate, 'b_gate': b_gate,
        'w_branch': w_branch, 'w_out': w_out, 'ln1_g': ln1_g, 'ln1_b': ln1_b,
        'w_ffn_in': w_ffn_in, 'w_ffn_out': w_ffn_out, 'w_ple_gate': w_ple_gate,
        'b_ple_gate': b_ple_gate, 'w_ple_proj': w_ple_proj, 'ln2_g': ln2_g, 'ln2_b': ln2_b,
    }
    y_prompt = run_trunk(x_prompt, p_prompt, emb_ln_g, emb_ln_b, prm)
    y_sample = run_trunk(x_sample, p_sample, emb_ln_g, emb_ln_b, prm)
    return (y_prompt, y_sample)
```

```python
import contextlib
import numpy as np
import concourse.bass as bass
import concourse.mybir as mybir
from concourse.bass_utils import run_bass_kernel_spmd

F32 = mybir.dt.float32
BF16 = mybir.dt.bfloat16
AF = mybir.ActivationFunctionType
ALU = mybir.AluOpType
AX = mybir.AxisListType

D = 1024
GRID_W = 64
FFN_H = 2816
PLE = 256
IN_COLS = 4384
ALPHA = 8 ** 0.25
LN_EPS = 1e-5
RMS_EPS = 1e-6
L2_EPS = 1e-6
NEG = -30000.0

ENGS = ['sync', 'scalar', 'vector', 'gpsimd', 'tensor']
EPOCH = 50000
NDMA = 24


class Prog:
    def __init__(self):
        self.ops = {e: [] for e in ENGS}
        self.count = {e: 0 for e in ENGS}
        self.known = {e: {} for e in ENGS}
        self.lastw = {}
        self.readers = {}
        self.dma_cnt = [0] * NDMA
        self.dma_rr = {'sync': 0, 'gpsimd': 0, 'scalar': 0}
        self.semkeys = set()
        self.nops = 0

    def _deps(self, reads, writes):
        toks = []
        for r in reads:
            t = self.lastw.get(r)
            if t is not None:
                toks.append(t)
        for w in writes:
            t = self.lastw.get(w)
            if t is not None:
                toks.append(t)
            rd = self.readers.get(w)
            if rd:
                toks.extend(rd.items())
        return toks

    def _filter(self, eng, toks):
        kn = self.known[eng]
        best = {}
        for (k, v) in toks:
            if kn.get(k, 0) >= v:
                continue
            if best.get(k, 0) < v:
                best[k] = v
        for k, v in best.items():
            kn[k] = v
        return list(best.items())

    def _commit(self, tok, reads, writes):
        k, v = tok
        for r in reads:
            d = self.readers.get(r)
            if d is None:
                d = {}
                self.readers[r] = d
            if d.get(k, 0) < v:
                d[k] = v
        for w in writes:
            self.lastw[w] = tok
            self.readers[w] = {}

    def op(self, eng, fn, reads=(), writes=()):
        toks = self._deps(reads, writes)
        n = self.count[eng] + 1
        self.count[eng] = n
        ep = (n - 1) // EPOCH
        key = ('e', eng, ep)
        self.semkeys.add(key)
        val = n - ep * EPOCH
        waits = self._filter(eng, toks)
        self.ops[eng].append((fn, waits, (key, 1)))
        self._commit((key, val), reads, writes)
        self.nops += 1

    def dma(self, eng, out, in_, reads=(), writes=(), **kw):
        lo, n = {'sync': (0, 14), 'gpsimd': (14, 10), 'scalar': (0, 14)}[eng]
        slot = lo + self.dma_rr[eng]
        self.dma_rr[eng] = (self.dma_rr[eng] + 1) % n
        prev = self.dma_cnt[slot]
        key = ('d', slot)
        self.semkeys.add(key)
        toks = self._deps(reads, writes)
        if prev > 0:
            toks.append((key, prev))
        self.dma_cnt[slot] = prev + 16
        waits = self._filter(eng, toks)
        self.ops[eng].append((lambda e: e.dma_start(out=out, in_=in_, **kw), waits, (key, 16)))
        self._commit((key, prev + 16), reads, writes)
        self.nops += 1

    def barrier(self):
        toks = []
        for e in ENGS:
            n = self.count[e]
            if n > 0:
                ep = (n - 1) // EPOCH
                toks.append((('e', e, ep), n - ep * EPOCH))
        for s in range(NDMA):
            if self.dma_cnt[s] > 0:
                toks.append((('d', s), self.dma_cnt[s]))
        for e in ENGS:
            waits = self._filter(e, toks)
            if waits:
                self.ops[e].append((None, waits, None))

    def finish(self):
        toks = []
        for e in ENGS:
            n = self.count[e]
            if n > 0:
                ep = (n - 1) // EPOCH
                toks.append((('e', e, ep), n - ep * EPOCH))
        for s in range(NDMA):
            if self.dma_cnt[s] > 0:
                toks.append((('d', s), self.dma_cnt[s]))
        waits = self._filter('sync', toks)
        self.ops['sync'].append((None, waits, None))

    def emit(self, nc):
        keys = sorted(self.semkeys, key=str)
        with contextlib.ExitStack() as st:
            sems = {}
            for i, k in enumerate(keys):
                sems[k] = st.enter_context(nc.semaphore("s%d" % i))
            block = st.enter_context(nc.Block())

            def mk(engname):
                def body(e):
                    for (fn, waits, inc) in self.ops[engname]:
                        for (k, v) in waits:
                            e.wait_ge(sems[k], v)
                        if fn is None:
                            continue
                        ins = fn(e)
                        if inc is not None:
                            ins.then_inc(sems[inc[0]], inc[1])
                return body
            block.sync(mk('sync'))
            block.scalar(mk('scalar'))
            block.vector(mk('vector'))
            block.gpsimd(mk('gpsimd'))
            block.tensor(mk('tensor'))


class Arena:
    def __init__(self, ap_f32, nbytes):
        self.ap = ap_f32
        self.nbytes = nbytes
        self.off = 0
        self.stack = []
        self.uid = 0

    def push(self):
        self.stack.append(self.off)

    def pop(self):
        self.off = self.stack.pop()

    def alloc(self, shape_free, dtype):
        if isinstance(shape_free, int):
            shape_free = (shape_free,)
        n = int(np.prod(shape_free))
        bpe = 4 if dtype == F32 else 2
        nb = (n * bpe + 63) // 64 * 64
        a = self.off
        assert a + nb <= self.nbytes, "SBUF arena overflow: need %d have %d" % (a + nb, self.nbytes)
        self.off = a + nb
        v = self.ap[:, a // 4:(a + nb) // 4]
        if dtype != F32:
            v = v.bitcast(dtype)
        v = v[:, 0:n]
        if len(shape_free) == 2:
            v = v.rearrange("p (a b) -> p a b", a=shape_free[0])
        elif len(shape_free) == 3:
            v = v.rearrange("p (a b c) -> p a b c", a=shape_free[0], b=shape_free[1])
        elif len(shape_free) == 4:
            v = v.rearrange("p (a b c d) -> p a b c d", a=shape_free[0], b=shape_free[1], c=shape_free[2])
        self.uid += 1
        return v


C_IDENT, C_ONES = 0, 128
C_UM = {'f': 256, 'b': 384}
C_NEGU = {'f': 512, 'b': 640}
C_USTR = {'f': 768, 'b': 896}
C_NEGONES = 1024
C_IND = {0: 1152, 1: 1280}
C_SEL = 1408
C_M1I = {'f': 1472, 'b': 1472 + 512}
C_M1S = {'f': 1472 + 1024, 'b': 1472 + 1536}
C_M2S = {'f': 1472 + 2048, 'b': 1472 + 2560}
NCONST = 1472 + 3072


def make_consts():
    c = np.zeros((128, NCONST), np.float32)
    t = np.arange(128)
    same = (t[:, None] // 64) == (t[None, :] // 64)
    c[:, C_IDENT:C_IDENT + 128] = np.eye(128)
    c[:, C_ONES:C_ONES + 128] = 1.0
    umf = ((t[:, None] <= t[None, :]) & same).astype(np.float32)
    umb = ((t[:, None] >= t[None, :]) & same).astype(np.float32)
    c[:, C_UM['f']:C_UM['f'] + 128] = umf
    c[:, C_UM['b']:C_UM['b'] + 128] = umb
    c[:, C_NEGU['f']:C_NEGU['f'] + 128] = -umf
    c[:, C_NEGU['b']:C_NEGU['b'] + 128] = -umb
    c[:, C_USTR['f']:C_USTR['f'] + 128] = ((t[:, None] > t[None, :]) & same)
    c[:, C_USTR['b']:C_USTR['b'] + 128] = ((t[:, None] < t[None, :]) & same)
    c[:, C_NEGONES:C_NEGONES + 128] = -1.0
    c[0:64, C_IND[0]:C_IND[0] + 128] = 1.0
    c[64:128, C_IND[1]:C_IND[1] + 128] = 1.0
    c[64, C_SEL:C_SEL + 64] = 1.0
    p = t[:, None]
    f = t[None, :]

    def msk(valid):
        m = np.where(valid & same, 0.0, NEG).astype(np.float32)
        return np.tile(m, (1, 4))
    c[:, C_M1I['f']:C_M1I['f'] + 512] = msk(f >= p)
    c[:, C_M1S['f']:C_M1S['f'] + 512] = msk(f > p)
    c[:, C_M2S['f']:C_M2S['f'] + 512] = msk(p > f)
    c[:, C_M1I['b']:C_M1I['b'] + 512] = msk(f <= p)
    c[:, C_M1S['b']:C_M1S['b'] + 512] = msk(f < p)
    c[:, C_M2S['b']:C_M2S['b'] + 512] = msk(p < f)
    return c


def make_rope(S):
    t = np.arange(S)
    row = (t // GRID_W).astype(np.float32)
    col = (t % GRID_W).astype(np.float32)
    inv = (np.float32(10000.0) ** (-np.arange(16, dtype=np.float32) / np.float32(16))).astype(np.float32)
    ar = row[:, None] * inv[None, :]
    ac = col[:, None] * inv[None, :]
    tab = np.concatenate([np.cos(ar), np.sin(ar), np.cos(ac), np.sin(ac)], axis=1)
    return tab.astype(np.float32)


def make_namask(S):
    rows = S // GRID_W
    nt = S // 128
    out = np.full((5, 128, 5, 128), NEG, np.float32)
    for gi in range(5):
        qb = None
        for cand in range(nt):
            ks = min(max(cand - 2, 0), nt - 5)
            if cand - ks == gi:
                qb = cand
                break
        if qb is None:
            continue
        ks = min(max(qb - 2, 0), nt - 5)
        for rl in range(2):
            r = 2 * qb + rl
            r0 = min(max(r - 4, 0), rows - 8)
            for c in range(64):
                c0 = min(max(c - 8, 0), GRID_W - 16)
                q = rl * 64 + c
                for j in range(5):
                    for krl in range(2):
                        kr = 2 * (ks + j) + krl
                        if kr < r0 or kr > r0 + 7:
                            continue
                        out[gi, krl * 64 + c0: krl * 64 + c0 + 16, j, q] = 0.0
    return out.reshape(5, 128, 640)


def expand_rpb(rpb):
    L = rpb.shape[0]
    kc = np.arange(64)[:, None]
    c = np.arange(64)[None, :]
    idx = kc - c + 15
    ok = (idx >= 0) & (idx <= 30)
    idxc = np.clip(idx, 0, 30)
    g = rpb[:, :, :, idxc]
    g = np.where(ok[None, None, None], g, np.float32(0.0))
    return np.ascontiguousarray(np.transpose(g, (0, 2, 3, 1, 4))).astype(np.float32)


WNAMES = [('emb_ln_g', (D,)), ('emb_ln_b', (D,)), ('w_in', ('L', D, IN_COLS)),
          ('att_q_norm_g', ('L', 64)), ('att_k_norm_g', ('L', 64)),
          ('dn_conv_w', ('L', 3, 1536)), ('dn_a_log', ('L', 2, 8)), ('dn_dt_bias', ('L', 2, 8)),
          ('dn_norm_g', ('L', 64)), ('w_gate', ('L', D, 3 * D)), ('b_gate', ('L', 3 * D)),
          ('w_branch', ('L', 3, 512, D)), ('w_out', ('L', D, D)), ('ln1_g', ('L', D)), ('ln1_b', ('L', D)),
          ('w_ffn_in', ('L', D, 2 * FFN_H)), ('w_ffn_out', ('L', FFN_H, D)),
          ('w_ple_gate', ('L', D, D)), ('b_ple_gate', ('L', D)), ('w_ple_proj', ('L', PLE, D)),
          ('ln2_g', ('L', D)), ('ln2_b', ('L', D))]


class Builder:
    def __init__(self, S, depth, nslot, debug=(), stages=None):
        self.S, self.L, self.NS = S, depth, nslot
        self.NT, self.NM = S // 128, S // 512
        self.debug = set(debug)
        self.stages = stages
        nc = bass.Bass("TRN2", target_bir_lowering=False)
        self.nc = nc
        self.P = Prog()
        self.es = contextlib.ExitStack()

        def inp(name, shape, dt=F32):
            return nc.dram_tensor(name, list(shape), dt, kind="ExternalInput").ap()
        self.x = inp("x", [nslot, S, D])
        self.p = inp("p", [depth, nslot, S, PLE])
        self.w = {}
        for name, shp in WNAMES:
            shp = [depth if s == 'L' else s for s in shp]
            self.w[name] = inp(name, shp)
        self.consts = inp("consts", [128, NCONST])
        self.rope = inp("rope", [S, 64])
        self.namask = inp("namask", [5, 128, 640])
        self.rpbT = inp("rpbT", [depth, 15, 64, 8, 64])
        self.y = nc.dram_tensor("y", [nslot, S, D], F32, kind="ExternalOutput").ap()

        def scr(name, shape, dt):
            kind = "ExternalOutput" if name in self.debug else "Internal"
            return nc.dram_tensor(name, list(shape), dt, kind=kind).ap()
        NT, NM = self.NT, self.NM
        self.xres = scr("xres", [NT, 128, D], F32)
        self.xT = scr("xT", [128, 8, S + 2], BF16)
        self.x1res = scr("x1res", [NT, 128, D], F32)
        self.x1T = scr("x1T", [NM, 128, 4096], BF16)
        self.qaT = scr("qaT", [NM, 128, 2048], BF16)
        self.kaT = scr("kaT", [NM, 128, 512], BF16)
        self.vaA = scr("vaA", [NT, 128, 130], BF16)
        self.braT = scr("braT", [NM, 64, 4096], BF16)
        self.nqT = scr("nqT", [NM, 128, 2048], BF16)
        self.nkT = scr("nkT", [NM, 128, 2048], BF16)
        self.nvA = scr("nvA", [NT, 128, 520], BF16)
        self.brbT = scr("brbT", [NM, 128, 2048], BF16)
        self.ctm = scr("ctm", [NT, 128, 2080], F32)
        self.ofwd = scr("ofwd", [NT, 128, 512], F32)
        self.brcT = scr("brcT", [NM, 128, 2048], BF16)
        self.aT = scr("aT", [NM, 128, 22 * 512], BF16)

        arena_t = self.es.enter_context(nc.sbuf_tensor("arena", [128, 48000], F32))
        psum_t = self.es.enter_context(nc.psum_tensor("psum", [128, 4096], F32))
        self.A = Arena(arena_t[:], 192000)
        self.ps = psum_t[:]
        self.uid = 0

    def end_stage(self):
        self.A.pop()
        self.P.barrier()

    def bank(self, b, n=512, off=0):
        return self.ps[:, b * 512 + off: b * 512 + off + n]

    def bankb(self, b):
        return self.ps[:, b * 512:(b + 1) * 512].bitcast(BF16)

    def op(self, eng, fn, reads=(), writes=()):
        self.P.op(eng, fn, reads, writes)

    def dma(self, eng, out, in_, reads=(), writes=(), **kw):
        self.P.dma(eng, out, in_, reads, writes, **kw)

    def load_w_bf16(self, dst, src, res, nsplit=8):
        K = dst.shape[1]
        sv = src.rearrange("(k p) n -> p k n", p=128)
        for k in range(K):
            self.dma('gpsimd', dst[:, k, :], sv[:, k, :], writes=[res + "_%d" % k])

    def wres(self, res, K):
        return [res + "_%d" % k for k in range(K)]

    def bcast_load(self, dst, src_row, res):
        self.dma('sync', dst, src_row.partition_broadcast(128), writes=[res])

    def setup_consts(self):
        A = self.A
        self.cf = A.alloc(NCONST, F32)
        self.dma('sync', self.cf, self.consts, writes=['cf'])
        self.identb = A.alloc(128, BF16)
        self.op('vector', lambda e: e.tensor_copy(out=self.identb, in_=self.cf[:, C_IDENT:C_IDENT + 128]),
                reads=['cf'], writes=['identb'])
        self.onesb = A.alloc(128, BF16)
        self.op('vector', lambda e: e.tensor_copy(out=self.onesb, in_=self.cf[:, C_ONES:C_ONES + 128]),
                reads=['cf'], writes=['onesb'])
        zp = A.alloc(8, BF16)
        self.op('gpsimd', lambda e: e.memset(zp, 0.0), writes=['zp'])
        self.dma('gpsimd', self.xT[:, :, 0:1], zp.unsqueeze(2), reads=['zp'], writes=['xTpad'], allow_slow_non_contiguous=True)
        self.dma('gpsimd', self.xT[:, :, self.S + 1:self.S + 2], zp.unsqueeze(2), reads=['zp'], writes=['xTpad'],
                 allow_slow_non_contiguous=True)
        self.lng = A.alloc(D, F32)
        self.lnb = A.alloc(D, F32)

    def c(self, off, n=128):
        return self.cf[:, off:off + n]

    def ln_tile(self, y, ry, out, rout, junk, rjunk, st, rst):
        op = self.op
        op('scalar', lambda e: e.activation(out=junk, in_=y, func=AF.Copy, accum_out=st[:, 0:1]),
           reads=[ry], writes=[rjunk, rst])
        op('vector', lambda e: e.tensor_scalar(out=st[:, 1:2], in0=st[:, 0:1], scalar1=-1.0 / D, scalar2=None,
                                               op0=ALU.mult), reads=[rst], writes=[rst])
        op('scalar', lambda e: e.activation(out=junk, in_=y, func=AF.Square, bias=st[:, 1:2],
                                            accum_out=st[:, 2:3]), reads=[ry, rst], writes=[rjunk, rst])
        op('scalar', lambda e: e.activation(out=st[:, 3:4], in_=st[:, 2:3], func=AF.Sqrt, scale=1.0 / D,
                                            bias=LN_EPS), reads=[rst], writes=[rst])
        op('vector', lambda e: e.reciprocal(out=st[:, 4:5], in_=st[:, 3:4]), reads=[rst], writes=[rst])
        op('vector', lambda e: e.tensor_scalar(out=out, in0=y, scalar1=st[:, 1:2], scalar2=st[:, 4:5],
                                               op0=ALU.add, op1=ALU.mult), reads=[ry, rst], writes=[rout])
        op('vector', lambda e: e.tensor_tensor(out=out, in0=out, in1=self.lng, op=ALU.mult),
           reads=[rout, 'lng'], writes=[rout])
        op('vector', lambda e: e.tensor_tensor(out=out, in0=out, in1=self.lnb, op=ALU.add),
           reads=[rout, 'lnb'], writes=[rout])

    def load_ln_params(self, g_row, b_row):
        self.bcast_load(self.lng, g_row, 'lng')
        self.bcast_load(self.lnb, b_row, 'lnb')

    def transpose_to_stage(self, xo, rxo, xb, rxb, stg, rstg, sub, pb):
        op = self.op
        op('scalar', lambda e: e.activation(out=xb, in_=xo, func=AF.Copy), reads=[rxo], writes=[rxb])
        pbv = self.bankb(pb)

        def tr(e):
            ins = None
            for k in range(8):
                ins = e.transpose(out=pbv[:, k * 128:(k + 1) * 128], in_=xb[:, k * 128:(k + 1) * 128],
                                  identity=self.identb)
            return ins
        op('tensor', tr, reads=[rxb, 'identb'], writes=['ps%d' % pb])
        op('vector', lambda e: e.tensor_copy(out=stg[:, :, sub * 128:(sub + 1) * 128],
                                             in_=pbv.rearrange("p (k n) -> p k n", k=8)),
           reads=['ps%d' % pb], writes=[rstg])

    def stage_embed_ln(self, slot, do_ln=True):
        A = self.A
        A.push()
        self.load_ln_params(self.w['emb_ln_g'].rearrange("(o n) -> o n", o=1),
                            self.w['emb_ln_b'].rearrange("(o n) -> o n", o=1))
        yb = [A.alloc(D, F32) for _ in range(2)]
        xo = [A.alloc(D, F32) for _ in range(2)]
        xb = [A.alloc(D, BF16) for _ in range(2)]
        junk = A.alloc(D, F32)
        st = [A.alloc(8, F32) for _ in range(2)]
        stg = [A.alloc((8, 512), BF16) for _ in range(2)]
        for mt in range(self.NM):
            sg = stg[mt % 2]
            rsg = 'e_stg%d' % (mt % 2)
            for sub in range(4):
                t = mt * 4 + sub
                i = t % 2
                self.dma('sync', yb[i], self.x[slot, t * 128:(t + 1) * 128, :], writes=['e_y%d' % i])
                if do_ln:
                    self.ln_tile(yb[i], 'e_y%d' % i, xo[i], 'e_xo%d' % i, junk, 'e_junk', st[i], 'e_st%d' % i)
                else:
                    self.op('vector', lambda e, i=i: e.tensor_copy(out=xo[i], in_=yb[i]), reads=['e_y%d' % i],
                            writes=['e_xo%d' % i])
                self.dma('gpsimd', self.xres[t], xo[i], reads=['e_xo%d' % i], writes=[('xres', t)])
                self.transpose_to_stage(xo[i], 'e_xo%d' % i, xb[i], 'e_xb%d' % i, sg, rsg, sub, 7)
            self.dma('gpsimd', self.xT[:, :, 1 + mt * 512: 1 + (mt + 1) * 512], sg, reads=[rsg], writes=[('xT', mt)])
        self.end_stage()

    def stage_p1a(self, l):
        A, op, dma = self.A, self.op, self.dma
        A.push()
        WA = A.alloc((8, 768), BF16)
        self.load_w_bf16(WA, self.w['w_in'][l][:, 0:768], 'WA')
        rWA = self.wres('WA', 8)
        gq = A.alloc(64, F32)
        gk = A.alloc(64, F32)
        self.bcast_load(gq, self.w['att_q_norm_g'][l:l + 1, :], 'gq')
        self.bcast_load(gk, self.w['att_k_norm_g'][l:l + 1, :], 'gk')
        G = A.alloc((10, 64), F32)
        op('vector', lambda e: e.tensor_scalar(out=G[:, 0:8, :], in0=gq.unsqueeze(1).to_broadcast([128, 8, 64]),
                                               scalar1=0.125, scalar2=None, op0=ALU.mult),
           reads=['gq'], writes=['G'])
        op('vector', lambda e: e.tensor_copy(out=G[:, 8:10, :], in_=gk.unsqueeze(1).to_broadcast([128, 2, 64])),
           reads=['gk', 'G'], writes=['G'])
        xTm = [A.alloc((8, 512), BF16) for _ in range(2)]
        tab = [A.alloc(64, F32) for _ in range(2)]
        sq = A.alloc(640, F32)
        ssq = A.alloc(16, F32)
        rstd = A.alloc(16, F32)
        qn = A.alloc((10, 64), F32)
        qro = A.alloc((10, 64), F32)
        tt = [A.alloc((10, 2, 16), F32) for _ in range(4)]
        qr = A.alloc((4, 2, 64), BF16)
        kr = A.alloc((2, 64), BF16)
        va = [A.alloc((2, 65), BF16) for _ in range(2)]
        for i in range(2):
            op('gpsimd', lambda e, i=i: e.memset(va[i], 1.0), writes=['a_va%d' % i])
        qstg = [A.alloc((4, 512), BF16) for _ in range(2)]
        kstg = [A.alloc(512, BF16) for _ in range(2)]
        pbv = self.bankb(4)
        for mt in range(self.NM):
            xm = xTm[mt % 2]
            rxm = 'a_xm%d' % (mt % 2)
            dma('sync', xm, self.xT[:, :, 1 + mt * 512: 1 + (mt + 1) * 512], reads=[('xT', mt)], writes=[rxm])
            qs, ks = qstg[mt % 2], kstg[mt % 2]
            rqs, rks = 'a_qs%d' % (mt % 2), 'a_ks%d' % (mt % 2)
            for sub in range(4):
                t = mt * 4 + sub
                i = t % 2
                b0 = 2 * i
                tb, rtb = tab[i], 'a_tab%d' % i
                dma('sync', tb, self.rope[t * 128:(t + 1) * 128, :], writes=[rtb])
                pq = self.ps[:, b0 * 512: b0 * 512 + 768]
                rps = ['ps%d' % b0, 'ps%d' % (b0 + 1)]

                def mm(e, xm=xm, sub=sub, pq=pq):
                    ins = None
                    for k in range(8):
                        lt = xm[:, k, sub * 128:(sub + 1) * 128]
                        e.matmul(pq[:, 0:512], lhsT=lt, rhs=WA[:, k, 0:512], start=(k == 0), stop=(k == 7))
                        ins = e.matmul(pq[:, 512:768], lhsT=lt, rhs=WA[:, k, 512:768], start=(k == 0), stop=(k == 7))
                    return ins
                op('tensor', mm, reads=[rxm] + rWA, writes=rps)
                op('scalar', lambda e, pq=pq: e.activation(out=sq, in_=pq[:, 0:640], func=AF.Square),
                   reads=rps, writes=['a_sq'])
                op('vector', lambda e: e.tensor_reduce(out=ssq[:, 0:10], in_=sq.rearrange("p (h d) -> p h d", d=64),
                                                       axis=AX.X, op=ALU.add), reads=['a_sq'], writes=['a_ssq'])
                op('scalar', lambda e: e.activation(out=rstd[:, 0:10], in_=ssq[:, 0:10], func=AF.Sqrt,
                                                    scale=1.0 / 64, bias=RMS_EPS), reads=['a_ssq'], writes=['a_rstd'])
                op('vector', lambda e: e.reciprocal(out=rstd[:, 0:10], in_=rstd[:, 0:10]),
                   reads=['a_rstd'], writes=['a_rstd'])
                op('vector', lambda e, pq=pq: e.tensor_tensor(
                    out=qn, in0=pq[:, 0:640].rearrange("p (h d) -> p h d", d=64),
                    in1=rstd[:, 0:10].unsqueeze(2).to_broadcast([128, 10, 64]), op=ALU.mult),
                   reads=rps + ['a_rstd'], writes=['a_qn'])
                op('vector', lambda e: e.tensor_tensor(out=qn, in0=qn, in1=G, op=ALU.mult),
                   reads=['a_qn', 'G'], writes=['a_qn'])
                q5 = qn.rearrange("p h (a b c) -> p h a b c", a=2, b=2)
                o5 = qro.rearrange("p h (a b c) -> p h a b c", a=2, b=2)
                X1, X2 = q5[:, :, :, 0, :], q5[:, :, :, 1, :]
                tv = tb.rearrange("p (a b c) -> p a b c", a=2, b=2)
                cs = tv[:, :, 0, :].unsqueeze(1).to_broadcast([128, 10, 2, 16])
                sn = tv[:, :, 1, :].unsqueeze(1).to_broadcast([128, 10, 2, 16])
                op('vector', lambda e, X1=X1, cs=cs: e.tensor_tensor(out=tt[0], in0=X1, in1=cs, op=ALU.mult),
                   reads=['a_qn', rtb], writes=['a_t0'])
                op('vector', lambda e, X2=X2, sn=sn: e.tensor_tensor(out=tt[1], in0=X2, in1=sn, op=ALU.mult),
                   reads=['a_qn', rtb], writes=['a_t1'])
                op('vector', lambda e, o5=o5: e.tensor_tensor(out=o5[:, :, :, 0, :], in0=tt[0], in1=tt[1], op=ALU.subtract),
                   reads=['a_t0', 'a_t1'], writes=['a_qro'])
                op('vector', lambda e, X2=X2, cs=cs: e.tensor_tensor(out=tt[2], in0=X2, in1=cs, op=ALU.mult),
                   reads=['a_qn', rtb], writes=['a_t2'])
                op('vector', lambda e, X1=X1, sn=sn: e.tensor_tensor(out=tt[3], in0=X1, in1=sn, op=ALU.mult),
                   reads=['a_qn', rtb], writes=['a_t3'])
                op('vector', lambda e, o5=o5: e.tensor_tensor(out=o5[:, :, :, 1, :], in0=tt[2], in1=tt[3], op=ALU.add),
                   reads=['a_t2', 'a_t3', 'a_qro'], writes=['a_qro'])
                op('vector', lambda e: e.tensor_copy(out=qr.rearrange("p m s d -> p s m d"),
                                                     in_=qro[:, 0:8, :].rearrange("p (s m) d -> p s m d", s=2)),
                   reads=['a_qro'], writes=['a_qr'])
                op('vector', lambda e: e.tensor_copy(out=kr, in_=qro[:, 8:10, :]), reads=['a_qro'], writes=['a_kr'])
                op('scalar', lambda e, pq=pq, i=i: e.activation(
                    out=va[i][:, :, 0:64], in_=pq[:, 640:768].rearrange("p (g d) -> p g d", g=2), func=AF.Copy),
                   reads=rps + ['a_va%d' % i], writes=['a_va%d' % i])
                dma('gpsimd', self.vaA[t], va[i].rearrange("p g c -> p (g c)"), reads=['a_va%d' % i],
                    writes=[('vaA', t)])

                def tr(e):
                    for m in range(4):
                        e.transpose(out=pbv[:, m * 128:(m + 1) * 128], in_=qr[:, m, :, :], identity=self.identb)
                    return e.transpose(out=pbv[:, 512:640], in_=kr, identity=self.identb)
                op('tensor', tr, reads=['a_qr', 'a_kr', 'identb'], writes=['ps4'])
                op('scalar', lambda e, qs=qs, sub=sub: e.activation(
                    out=qs[:, :, sub * 128:(sub + 1) * 128], in_=pbv[:, 0:512].rearrange("p (m n) -> p m n", m=4),
                    func=AF.Copy), reads=['ps4'], writes=[rqs])
                op('scalar', lambda e, ks=ks, sub=sub: e.activation(
                    out=ks[:, sub * 128:(sub + 1) * 128], in_=pbv[:, 512:640], func=AF.Copy),
                   reads=['ps4'], writes=[rks])
            dma('gpsimd', self.qaT[mt], qs.rearrange("p m n -> p (m n)"), reads=[rqs], writes=[('qaT', mt)])
            dma('gpsimd', self.kaT[mt], ks, reads=[rks], writes=[('kaT', mt)])
        self.end_stage()

    def stage_attn(self):
        A, op, dma = self.A, self.op, self.dma
        S, NT, NM = self.S, self.NT, self.NM
        A.push()
        KT = A.alloc(S, BF16)
        VA = A.alloc((NT, 130), BF16)
        for mt in range(NM):
            dma('sync', KT[:, mt * 512:(mt + 1) * 512], self.kaT[mt], reads=[('kaT', mt)], writes=['A_KT%d' % mt])
        rKT = ['A_KT%d' % mt for mt in range(NM)]
        TCH = 16
        rVA = []
        for t0 in range(0, NT, TCH):
            t1 = min(NT, t0 + TCH)
            dma('sync', VA[:, t0:t1, :], self.vaA[t0:t1].rearrange("t p c -> p t c"),
                reads=[('vaA', t) for t in range(t0, t1)], writes=['A_VA%d' % t0])
            rVA.append('A_VA%d' % t0)
        Qm = [A.alloc((4, 512), BF16) for _ in range(2)]
        ET = [A.alloc(1024, BF16) for _ in range(3)]
        Osb = A.alloc((2, 512), F32)
        rec = A.alloc(512, F32)
        bst = [A.alloc((8, 512), BF16) for _ in range(2)]
        sel = self.cf[0:65, C_SEL:C_SEL + 64]
        cnt = 0
        for mt in range(NM):
            qm, rqm = Qm[mt % 2], 'A_qm%d' % (mt % 2)
            dma('sync', qm.rearrange("p m n -> p (m n)"), self.qaT[mt], reads=[('qaT', mt)], writes=[rqm])
            bs, rbs = bst[mt % 2], 'A_bs%d' % (mt % 2)
            for m in range(4):
                def ST(kt, par, m=m, qm=qm, rqm=rqm):
                    def f(e):
                        e.matmul(self.ps[:, par * 1024: par * 1024 + 512], lhsT=KT[0:64, kt * 128:(kt + 1) * 128],
                                 rhs=qm[0:64, m, :], start=True, stop=True)
                        return e.matmul(self.ps[:, par * 1024 + 512: par * 1024 + 1024],
                                        lhsT=KT[64:128, kt * 128:(kt + 1) * 128], rhs=qm[64:128, m, :],
                                        start=True, stop=True)
                    op('tensor', f, reads=[rqm] + rKT, writes=['ps%d' % (2 * par), 'ps%d' % (2 * par + 1)])
                ST(0, 0)
                for kt in range(NT):
                    par = kt % 2
                    if kt + 1 < NT:
                        ST(kt + 1, (kt + 1) % 2)
                    ei = cnt % 3
                    cnt += 1
                    et, ret = ET[ei], 'A_et%d' % ei
                    op('scalar', lambda e, et=et, par=par: e.activation(
                        out=et, in_=self.ps[:, par * 1024:(par + 1) * 1024], func=AF.Exp),
                       reads=['ps%d' % (2 * par), 'ps%d' % (2 * par + 1)], writes=[ret])

                    def PV(e, kt=kt, et=et):
                        e.matmul(self.ps[0:65, 4 * 512:5 * 512], lhsT=VA[:, kt, 0:65], rhs=et[:, 0:512],
                                 start=(kt == 0), stop=(kt == NT - 1))
                        return e.matmul(self.ps[0:65, 5 * 512:6 * 512], lhsT=VA[:, kt, 65:130], rhs=et[:, 512:1024],
                                        start=(kt == 0), stop=(kt == NT - 1))
                    op('tensor', PV, reads=[ret] + rVA, writes=['ps4', 'ps5'])
                for s in range(2):
                    h = m + 4 * s
                    op('scalar', lambda e, s=s: e.activation(out=Osb[0:65, s, :], in_=self.ps[0:65, (4 + s) * 512:(5 + s) * 512],
                                                             func=AF.Copy), reads=['ps%d' % (4 + s)], writes=['A_osb%d' % s])
                    op('tensor', lambda e, s=s: e.matmul(self.ps[0:64, 6 * 512:7 * 512], lhsT=sel, rhs=Osb[0:65, s, :],
                                                         start=True, stop=True),
                       reads=['A_osb%d' % s, 'cf'], writes=['ps6'])
                    op('vector', lambda e: e.reciprocal(out=rec[0:64, :], in_=self.ps[0:64, 6 * 512:7 * 512]),
                       reads=['ps6'], writes=['A_rec'])
                    op('vector', lambda e, s=s, h=h, bs=bs: e.tensor_tensor(out=bs[0:64, h, :], in0=Osb[0:64, s, :],
                                                                           in1=rec[0:64, :], op=ALU.mult),
                       reads=['A_osb%d' % s, 'A_rec'], writes=[rbs])
            dma('gpsimd', self.braT[mt], bs[0:64].rearrange("p h n -> p (h n)"), reads=[rbs], writes=[('braT', mt)])
        self.end_stage()

    def stage_p1b(self, l):
        A, op, dma = self.A, self.op, self.dma
        A.push()
        WB = A.alloc((8, 1536), BF16)
        self.load_w_bf16(WB, self.w['w_in'][l][:, 768:2304], 'WB')
        rWB = self.wres('WB', 8)
        xTm = [A.alloc((8, 512), BF16) for _ in range(2)]
        qstg = [A.alloc((4, 512), BF16) for _ in range(2)]
        kstg = [A.alloc((4, 512), BF16) for _ in range(2)]
        nv = [A.alloc((8, 65), BF16) for _ in range(2)]
        for i in range(2):
            op('gpsimd', lambda e, i=i: e.memset(nv[i], 1.0), writes=['b_nv%d' % i])
        cnt = 0
        for mt in range(self.NM):
            xm, rxm = xTm[mt % 2], 'b_xm%d' % (mt % 2)
            dma('sync', xm, self.xT[:, :, 1 + mt * 512: 1 + (mt + 1) * 512], reads=[('xT', mt)], writes=[rxm])
            qs, ks = qstg[mt % 2], kstg[mt % 2]
            rqs, rks = 'b_qs%d' % (mt % 2), 'b_ks%d' % (mt % 2)
            for c in range(8):
                b = cnt % 4
                cnt += 1

                def mm(e, xm=xm, c=c, b=b):
                    ins = None
                    for k in range(8):
                        ins = e.matmul(self.bank(b), lhsT=WB[:, k, c * 128:(c + 1) * 128], rhs=xm[:, k, :],
                                       start=(k == 0), stop=(k == 7))
                    return ins
                op('tensor', mm, reads=[rxm] + rWB, writes=['ps%d' % b])
                if c < 4:
                    op('scalar', lambda e, qs=qs, c=c, b=b: e.activation(out=qs[:, c, :], in_=self.bank(b),
                                                                         func=AF.Copy, scale=0.125),
                       reads=['ps%d' % b], writes=[rqs])
                else:
                    op('vector', lambda e, ks=ks, c=c, b=b: e.tensor_copy(out=ks[:, c - 4, :], in_=self.bank(b)),
                       reads=['ps%d' % b], writes=[rks])
            for sub in range(4):
                t = mt * 4 + sub
                i = t % 2
                b = 4 + i

                def mv(e, xm=xm, sub=sub, b=b):
                    ins = None
                    for k in range(8):
                        ins = e.matmul(self.bank(b), lhsT=xm[:, k, sub * 128:(sub + 1) * 128], rhs=WB[:, k, 1024:1536],
                                       start=(k == 0), stop=(k == 7))
                    return ins
                op('tensor', mv, reads=[rxm] + rWB, writes=['ps%d' % b])
                op('vector', lambda e, i=i, b=b: e.tensor_copy(
                    out=nv[i][:, :, 0:64], in_=self.bank(b).rearrange("p (h d) -> p h d", h=8)),
                   reads=['ps%d' % b, 'b_nv%d' % i], writes=['b_nv%d' % i])
                dma('gpsimd', self.nvA[t], nv[i].rearrange("p h c -> p (h c)"), reads=['b_nv%d' % i],
                    writes=[('nvA', t)])
            dma('gpsimd', self.nqT[mt], qs.rearrange("p m n -> p (m n)"), reads=[rqs], writes=[('nqT', mt)])
            dma('gpsimd', self.nkT[mt], ks.rearrange("p m n -> p (m n)"), reads=[rks], writes=[('nkT', mt)])
        self.end_stage()

    def stage_na(self, l):
        A, op, dma = self.A, self.op, self.dma
        S, NT, NM = self.S, self.NT, self.NM
        A.push()
        KT = A.alloc((4, S), BF16)
        rKT = []
        for mt in range(NM):
            dma('sync', KT[:, :, mt * 512:(mt + 1) * 512], self.nkT[mt].rearrange("p (m n) -> p m n", m=4),
                reads=[('nkT', mt)], writes=['B_KT%d' % mt])
            rKT.append('B_KT%d' % mt)
        VR = [A.alloc(520, BF16) for _ in range(8)]
        BMs = A.alloc((8, 5, 128), BF16)
        BMx = A.alloc((8, 5, 128), BF16)
        bstage = A.alloc((8, 5, 128), F32)
        mk = A.alloc(640, F32)
        Qm = [A.alloc((4, 512), BF16) for _ in range(2)]
        ET = [A.alloc((5, 128), BF16) for _ in range(3)]
        rec = A.alloc(8, F32)
        ob = A.alloc((8, 64), BF16)
        bstg = [A.alloc((4, 512), BF16) for _ in range(2)]
        pbv = self.bankb(6)

        def build_bm(gi, dst, rdst):
            op('gpsimd', lambda e: e.memset(bstage, 0.0), writes=['B_bst'])
            dma('sync', mk, self.namask[gi], writes=['B_mk'])
            for j in range(5):
                for krl in range(2):
                    for rl in range(2):
                        dr = 2 * j + krl - 2 * gi - rl + 7
                        if dr < 0 or dr > 14:
                            continue
                        dma('sync', bstage[krl * 64:(krl + 1) * 64, :, j, rl * 64:(rl + 1) * 64],
                            self.rpbT[l, dr], reads=['B_bst'], writes=['B_bst'])
            op('vector', lambda e: e.tensor_tensor(
                out=dst.rearrange("p h j q -> p h (j q)"), in0=bstage.rearrange("p h j q -> p h (j q)"),
                in1=mk.unsqueeze(1).to_broadcast([128, 8, 640]), op=ALU.add),
               reads=['B_bst', 'B_mk'], writes=[rdst])

        loaded = 0
        cnt = 0
        built_std = False
        for qb in range(NT):
            mt, sub = qb // 4, qb % 4
            ks0 = min(max(qb - 2, 0), NT - 5)
            gi = qb - ks0
            if gi == 2:
                if not built_std:
                    build_bm(2, BMs, 'B_BMs')
                    built_std = True
                BM, rBM = BMs, 'B_BMs'
            else:
                build_bm(gi, BMx, 'B_BMx')
                BM, rBM = BMx, 'B_BMx'
            qm, rqm = Qm[mt % 2], 'B_qm%d' % (mt % 2)
            if sub == 0:
                dma('sync', qm.rearrange("p m n -> p (m n)"), self.nqT[mt], reads=[('nqT', mt)], writes=[rqm])
            while loaded < ks0 + 5:
                dma('sync', VR[loaded % 8], self.nvA[loaded], reads=[('nvA', loaded)], writes=['B_VR%d' % (loaded % 8)])
                loaded += 1
            bs, rbs = bstg[mt % 2], 'B_bs%d' % (mt % 2)

            def ST(h, par, qm=qm, rqm=rqm, sub=sub, ks0=ks0, BM=BM, rBM=rBM):
                m, sl = h // 2, h % 2
                r0 = sl * 64

                def f(e):
                    ins = None
                    for j in range(5):
                        o = self.ps[:, par * 1024 + j * 128: par * 1024 + (j + 1) * 128]
                        e.matmul(o, lhsT=KT[r0:r0 + 64, m, (ks0 + j) * 128:(ks0 + j + 1) * 128],
                                 rhs=qm[r0:r0 + 64, m, sub * 128:(sub + 1) * 128], start=True, stop=False)
                        ins = e.matmul(o, lhsT=self.identb, rhs=BM[:, h, j, :], start=False, stop=True)
                    return ins
                op('tensor', f, reads=[rqm, rBM, 'identb'] + rKT, writes=['ps%d' % (2 * par), 'ps%d' % (2 * par + 1)])
            ST(0, 0)
            for h in range(8):
                par = h % 2
                if h + 1 < 8:
                    ST(h + 1, (h + 1) % 2)
                ei = cnt % 3
                cnt += 1
                et, ret = ET[ei], 'B_et%d' % ei
                op('scalar', lambda e, et=et, par=par: e.activation(
                    out=et.rearrange("p j q -> p (j q)"), in_=self.ps[:, par * 1024: par * 1024 + 640], func=AF.Exp),
                   reads=['ps%d' % (2 * par), 'ps%d' % (2 * par + 1)], writes=[ret])

                def PV(e, h=h, et=et, ks0=ks0):
                    ins = None
                    ob_ = (4 + h // 4) * 512 + (h % 4) * 65
                    for j in range(5):
                        ins = e.matmul(self.ps[:, ob_: ob_ + 65], lhsT=et[:, j, :],
                                       rhs=VR[(ks0 + j) % 8][:, h * 65:(h + 1) * 65], start=(j == 0), stop=(j == 4))
                    return ins
                op('tensor', PV, reads=[ret] + ['B_VR%d' % ((ks0 + j) % 8) for j in range(5)],
                   writes=['ps%d' % (4 + h // 4)])
            for hf in range(2):
                pv = self.ps[:, (4 + hf) * 512:(4 + hf) * 512 + 260].rearrange("p (h c) -> p h c", h=4)
                op('vector', lambda e, pv=pv, hf=hf: e.reciprocal(out=rec[:, hf * 4:(hf + 1) * 4].unsqueeze(2),
                                                                 in_=pv[:, :, 64:65]),
                   reads=['ps%d' % (4 + hf)], writes=['B_rec'])
                op('vector', lambda e, pv=pv, hf=hf: e.tensor_tensor(
                    out=ob[:, hf * 4:(hf + 1) * 4, :], in0=pv[:, :, 0:64],
                    in1=rec[:, hf * 4:(hf + 1) * 4].unsqueeze(2).to_broadcast([128, 4, 64]), op=ALU.mult),
                   reads=['ps%d' % (4 + hf), 'B_rec'], writes=['B_ob'])

            def tr(e):
                ins = None
                for m in range(4):
                    ins = e.transpose(out=pbv[:, m * 128:(m + 1) * 128], in_=ob[:, 2 * m:2 * m + 2, :], identity=self.identb)
                return ins
            op('tensor', tr, reads=['B_ob', 'identb'], writes=['ps6'])
            op('scalar', lambda e, bs=bs, sub=sub: e.activation(
                out=bs[:, :, sub * 128:(sub + 1) * 128], in_=pbv[:, 0:512].rearrange("p (m n) -> p m n", m=4),
                func=AF.Copy), reads=['ps6'], writes=[rbs])
            if sub == 3:
                dma('gpsimd', self.brbT[mt], bs.rearrange("p m n -> p (m n)"), reads=[rbs], writes=[('brbT', mt)])
        self.end_stage()

    def stage_p1c(self, l):
        A, op, dma = self.A, self.op, self.dma
        NM = self.NM
        A.push()
        WC = [A.alloc((8, 1536), BF16) for _ in range(3)]
        WZ = A.alloc((8, 544), BF16)
        self.load_w_bf16(WZ, self.w['w_in'][l][:, 3840:4384], 'WZ')
        rWZ = self.wres('WZ', 8)
        cwb = A.alloc((3, 1536), F32)
        for j in range(3):
            dma('sync', cwb[:, j, :], self.w['dn_conv_w'][l, j:j + 1, :].partition_broadcast(128), writes=['c_cwb%d' % j])
        wst = [A.alloc(1536, F32) for _ in range(2)]
        for k in range(8):
            i = k % 2
            dma('sync', wst[i], self.w['w_in'][l][k * 128:(k + 1) * 128, 2304:3840], writes=['c_wst%d' % i])
            for j in range(3):
                eng = 'gpsimd' if j < 2 else 'vector'
                op(eng, lambda e, i=i, j=j, k=k: e.tensor_tensor(out=WC[j][:, k, :], in0=wst[i], in1=cwb[:, j, :],
                                                                op=ALU.mult),
                   reads=['c_wst%d' % i, 'c_cwb%d' % j], writes=['c_WC%d_%d' % (j, k)])
        rWC = ['c_WC%d_%d' % (j, k) for j in range(3) for k in range(8)]
        dtb = A.alloc(16, F32)
        nega = A.alloc(16, F32)
        dma('sync', dtb, self.w['dn_dt_bias'][l:l + 1].rearrange("o a b -> o (a b)").partition_broadcast(128),
            writes=['c_dtb'])
        dma('sync', nega, self.w['dn_a_log'][l:l + 1].rearrange("o a b -> o (a b)").partition_broadcast(128),
            writes=['c_nega'])
        op('scalar', lambda e: e.activation(out=nega, in_=nega, func=AF.Exp), reads=['c_nega'], writes=['c_nega'])
        op('vector', lambda e: e.tensor_scalar(out=nega, in0=nega, scalar1=-1.0, scalar2=None, op0=ALU.mult),
           reads=['c_nega'], writes=['c_nega'])
        xs = [A.alloc((8, 512), BF16) for _ in range(3)]
        TM = [A.alloc(2080, F32) for _ in range(2)]
        sq = A.alloc(1024, F32)
        ssq = A.alloc(16, F32)
        rn = A.alloc(16, F32)
        t16 = A.alloc(16, F32)
        for mt in range(NM):
            rxm = 'c_xs'
            rd = [('xT', m2) for m2 in (mt - 1, mt, mt + 1) if 0 <= m2 < NM] + ['xTpad']
            for j in range(3):
                dma('sync', xs[j], self.xT[:, :, mt * 512 + j: mt * 512 + j + 512], reads=rd, writes=['c_xs%d' % j])
            rxs = ['c_xs0', 'c_xs1', 'c_xs2']
            for sub in range(4):
                t = mt * 4 + sub
                i = t % 2
                cb = 3 * i
                tm, rtm = TM[i], 'c_tm%d' % i
                rcb = ['ps%d' % (cb + n) for n in range(3)]

                def mmc(e, sub=sub, cb=cb):
                    ins = None
                    for n in range(3):
                        for j in range(3):
                            for k in range(8):
                                ins = e.matmul(self.bank(cb + n), lhsT=xs[j][:, k, sub * 128: sub * 128 + 128],
                                               rhs=WC[j][:, k, n * 512:(n + 1) * 512],
                                               start=(j == 0 and k == 0), stop=(j == 2 and k == 7))
                    return ins
                op('tensor', mmc, reads=rxs + rWC, writes=rcb)

                def mmz(e, sub=sub):
                    ins = None
                    for k in range(8):
                        lt = xs[1][:, k, sub * 128: sub * 128 + 128]
                        e.matmul(self.bank(6), lhsT=lt, rhs=WZ[:, k, 0:512], start=(k == 0), stop=(k == 7))
                        ins = e.matmul(self.bank(7, 32), lhsT=lt, rhs=WZ[:, k, 512:544], start=(k == 0), stop=(k == 7))
                    return ins
                op('tensor', mmz, reads=rxs + rWZ, writes=['ps6', 'ps7'])
                op('scalar', lambda e, tm=tm, cb=cb: e.activation(out=tm[:, 0:1024], in_=self.ps[:, cb * 512:(cb + 2) * 512],
                                                                  func=AF.Silu), reads=rcb, writes=[rtm])
                op('scalar', lambda e, tm=tm, cb=cb: e.activation(out=tm[:, 1024:1536], in_=self.bank(cb + 2),
                                                                  func=AF.Silu), reads=rcb + [rtm], writes=[rtm])
                op('scalar', lambda e, tm=tm: e.activation(out=sq, in_=tm[:, 0:1024], func=AF.Square),
                   reads=[rtm], writes=['c_sq'])
                op('vector', lambda e: e.tensor_reduce(out=ssq, in_=sq.rearrange("p (h d) -> p h d", d=64), axis=AX.X,
                                                       op=ALU.add), reads=['c_sq'], writes=['c_ssq'])
                op('scalar', lambda e: e.activation(out=rn, in_=ssq, func=AF.Sqrt, bias=L2_EPS), reads=['c_ssq'],
                   writes=['c_rn'])
                op('vector', lambda e: e.reciprocal(out=rn, in_=rn), reads=['c_rn'], writes=['c_rn'])
                op('vector', lambda e: e.tensor_scalar(out=rn[:, 0:8], in0=rn[:, 0:8], scalar1=0.125, scalar2=None,
                                                       op0=ALU.mult), reads=['c_rn'], writes=['c_rn'])
                op('vector', lambda e, tm=tm: e.tensor_tensor(
                    out=tm[:, 0:1024].rearrange("p (h d) -> p h d", d=64),
                    in0=tm[:, 0:1024].rearrange("p (h d) -> p h d", d=64),
                    in1=rn.unsqueeze(2).to_broadcast([128, 16, 64]), op=ALU.mult), reads=[rtm, 'c_rn'], writes=[rtm])
                op('scalar', lambda e, tm=tm: e.activation(out=tm[:, 1536:2048], in_=self.bank(6), func=AF.Silu),
                   reads=['ps6', rtm], writes=[rtm])
                op('scalar', lambda e, tm=tm: e.activation(out=tm[:, 2064:2080], in_=self.bank(7, 16, 16), func=AF.Sigmoid),
                   reads=['ps7', rtm], writes=[rtm])
                op('vector', lambda e: e.tensor_tensor(out=t16, in0=self.bank(7, 16), in1=dtb, op=ALU.add),
                   reads=['ps7', 'c_dtb'], writes=['c_t16'])
                op('scalar', lambda e: e.activation(out=t16, in_=t16, func=AF.Exp), reads=['c_t16'], writes=['c_t16'])
                op('scalar', lambda e: e.activation(out=t16, in_=t16, func=AF.Ln, bias=1.0), reads=['c_t16'], writes=['c_t16'])
                op('vector', lambda e, tm=tm: e.tensor_tensor(out=tm[:, 2048:2064], in0=t16, in1=nega, op=ALU.mult),
                   reads=['c_t16', 'c_nega', rtm], writes=[rtm])
                dma('gpsimd', self.ctm[t], tm, reads=[rtm], writes=[('ctm', t)])
        self.end_stage()

    def stage_cpass(self, l, d):
        A, op, dma, ps = self.A, self.op, self.dma, self.ps
        NT = self.NT
        di = 0 if d == 'f' else 1
        A.push()
        c = self.c
        TMs = [A.alloc(2080, F32) for _ in range(2)]
        gam = A.alloc(16, F32)
        gtp = A.alloc((2, 8), F32)
        bg = A.alloc(8, F32)
        tok = A.alloc((7, 512), BF16)
        TR = A.alloc((32, 128), BF16)
        rhsA = A.alloc((8, 128), F32)
        rhsB = A.alloc((8, 128), F32)
        E = [A.alloc((8, 128), F32) for _ in range(3)]
        Pm = [A.alloc((8, 128), F32) for _ in range(2)]
        CB = [A.alloc((8, 2, 128), F32) for _ in range(2)]
        IT = A.alloc((8, 128), BF16)
        TTb = A.alloc((8, 128), BF16)
        usb = A.alloc(512, F32)
        wTb = A.alloc((8, 128), BF16)
        S32 = A.alloc((8, 64), F32)
        Sb = A.alloc((8, 64), BF16)
        vnb = A.alloc((8, 64), BF16)
        osb = A.alloc(512, F32)
        I8 = A.alloc((8, 128), F32)
        for h in range(8):
            op('vector', lambda e, h=h: e.tensor_copy(out=I8[:, h, :], in_=c(C_IDENT)), reads=['cf'], writes=['C_I8'])
        op('gpsimd', lambda e: e.memset(S32, 0.0), writes=['C_S32'])
        op('gpsimd', lambda e: e.memset(Sb, 0.0), writes=['C_Sb'])
        if d == 'b':
            ofw = A.alloc(512, F32)
            sq = A.alloc(512, F32)
            r8 = A.alloc(8, F32)
            gn = A.alloc(64, F32)
            self.bcast_load(gn, self.w['dn_norm_g'][l:l + 1, :], 'C_gn')
            brc = A.alloc(512, BF16)
            cstg = [A.alloc((4, 512), BF16) for _ in range(2)]
        order = range(NT) if d == 'f' else range(NT - 1, -1, -1)
        MSK = [C_M1I[d], C_M1S[d], C_M2S[d]]
        for t in order:
            i = t % 2
            tm, rtm = TMs[i], 'C_tm%d' % i
            dma('sync', tm, self.ctm[t], reads=[('ctm', t)], writes=[rtm])
            g = tm[:, 2048 + 8 * di: 2056 + 8 * di]
            beta = tm[:, 2064 + 8 * di: 2072 + 8 * di]
            qv = tm[:, 0:512].rearrange("p (h d) -> p h d", d=64)
            kv = tm[:, 512:1024].rearrange("p (h d) -> p h d", d=64)
            vv = tm[:, 1024:1536].rearrange("p (h d) -> p h d", d=64)

            def gsm(e, g=g):
                e.matmul(ps[:, 7 * 512: 7 * 512 + 8], lhsT=c(C_UM[d]), rhs=g, start=True, stop=True)
                e.matmul(ps[:, 7 * 512 + 8: 7 * 512 + 16], lhsT=c(C_USTR[d]), rhs=g, start=True, stop=True)
                e.matmul(ps[:, 7 * 512 + 16: 7 * 512 + 24], lhsT=c(C_IND[0]), rhs=g, start=True, stop=True)
                return e.matmul(ps[:, 7 * 512 + 24: 7 * 512 + 32], lhsT=c(C_IND[1]), rhs=g, start=True, stop=True)
            op('tensor', gsm, reads=[rtm, 'cf'], writes=['ps7'])
            op('scalar', lambda e: e.activation(out=gam, in_=ps[:, 7 * 512: 7 * 512 + 16], func=AF.Exp),
               reads=['ps7'], writes=['C_gam'])
            op('scalar', lambda e: e.activation(out=gtp[0:64].rearrange("p c h -> p (c h)"),
                                                in_=ps[0:64, 7 * 512 + 16: 7 * 512 + 32], func=AF.Exp),
               reads=['ps7'], writes=['C_gtp'])
            op('vector', lambda e, beta=beta: e.tensor_tensor(out=bg, in0=beta, in1=gam[:, 0:8], op=ALU.mult),
               reads=[rtm, 'C_gam'], writes=['C_bg'])
            if getattr(self, 'cstop', 99) <= 2:
                continue
            tk = lambda n: tok[:, n, :].rearrange("p (h d) -> p h d", d=64)
            bc = lambda a: a.unsqueeze(2).to_broadcast([128, 8, 64])
            op('scalar', lambda e, kv=kv: e.activation(out=tk(0), in_=kv, func=AF.Copy), reads=[rtm], writes=['C_tok0'])
            op('scalar', lambda e, qv=qv: e.activation(out=tk(1), in_=qv, func=AF.Copy), reads=[rtm], writes=['C_tok1'])
            op('vector', lambda e, kv=kv, beta=beta: e.tensor_tensor(out=tk(2), in0=kv, in1=bc(beta), op=ALU.mult),
               reads=[rtm], writes=['C_tok2'])
            op('vector', lambda e, vv=vv, beta=beta: e.tensor_tensor(out=tk(3), in0=vv, in1=bc(beta), op=ALU.mult),
               reads=[rtm], writes=['C_tok3'])
            op('vector', lambda e, kv=kv: e.tensor_tensor(out=tk(4), in0=kv, in1=bc(bg), op=ALU.mult),
               reads=[rtm, 'C_bg'], writes=['C_tok4'])
            op('vector', lambda e, kv=kv: e.tensor_tensor(out=tk(5), in0=kv, in1=bc(gam[:, 8:16]), op=ALU.mult),
               reads=[rtm, 'C_gam'], writes=['C_tok5'])
            op('vector', lambda e, qv=qv: e.tensor_tensor(out=tk(6), in0=qv, in1=bc(gam[:, 0:8]), op=ALU.mult),
               reads=[rtm, 'C_gam'], writes=['C_tok6'])
            if getattr(self, 'cstop', 99) <= 4:
                continue
            pb03 = ps[:, 0:2048].bitcast(BF16)

            def trs(e):
                ins = None
                for a, n in enumerate((0, 2, 1, 6)):
                    for h in range(8):
                        ins = e.transpose(out=pb03[0:64, (a * 8 + h) * 128:(a * 8 + h + 1) * 128],
                                          in_=tok[:, n, h * 64:(h + 1) * 64], identity=self.identb)
                return ins
            op('tensor', trs, reads=['C_tok0', 'C_tok2', 'C_tok1', 'C_tok6', 'identb'], writes=['ps0', 'ps1', 'ps2', 'ps3'])
            for q in range(2):
                op('scalar', lambda e, q=q: e.activation(out=TR[0:64, 16 * q:16 * q + 16, :].rearrange("p a n -> p (a n)"),
                                                         in_=pb03[0:64, 2048 * q:2048 * (q + 1)], func=AF.Copy),
                   reads=['ps0', 'ps1', 'ps2', 'ps3', 'C_TR'], writes=['C_TR'])
            if getattr(self, 'cstop', 99) <= 5:
                continue
            op('vector', lambda e, g=g: e.tensor_tensor(
                out=rhsA, in0=c(C_UM[d]).unsqueeze(1).to_broadcast([128, 8, 128]),
                in1=g.unsqueeze(2).to_broadcast([128, 8, 128]), op=ALU.mult), reads=[rtm, 'cf'], writes=['C_rhsA'])
            op('vector', lambda e, g=g: e.tensor_copy(out=rhsB, in_=g.unsqueeze(2).to_broadcast([128, 8, 128])),
               reads=[rtm], writes=['C_rhsB'])
            gcnt = 0
            for gi in range(3):
                for hf in range(2):
                    b = 2 + (gcnt % 2)
                    gcnt += 1
                    l1, l2 = (c(C_ONES), c(C_NEGU[d])) if gi < 2 else (c(C_NEGONES), c(C_UM[d]))

                    def dg(e, b=b, hf=hf, l1=l1, l2=l2, gi=gi):
                        e.matmul(self.bank(b), lhsT=l1, rhs=rhsA[:, 4 * hf:4 * hf + 4, :].rearrange("p h f -> p (h f)"),
                                 start=True, stop=False)
                        e.matmul(self.bank(b), lhsT=l2, rhs=rhsB[:, 4 * hf:4 * hf + 4, :].rearrange("p h f -> p (h f)"),
                                 start=False, stop=False)
                        return e.matmul(self.bank(b), lhsT=c(C_IDENT), rhs=c(MSK[gi], 512), start=False, stop=True)
                    op('tensor', dg, reads=['C_rhsA', 'C_rhsB', 'cf'], writes=['ps%d' % b])
                    op('scalar', lambda e, b=b, hf=hf, gi=gi: e.activation(
                        out=E[gi][:, 4 * hf:4 * hf + 4, :].rearrange("p h f -> p (h f)"), in_=self.bank(b), func=AF.Exp),
                       reads=['ps%d' % b], writes=['C_E%d' % gi])
            if getattr(self, 'cstop', 99) <= 6:
                continue
            def gram(a_l, a_r, base):
                def f(e):
                    ins = None
                    for h in range(8):
                        ins = e.matmul(ps[:, base + h * 128: base + (h + 1) * 128], lhsT=TR[0:64, a_l * 8 + h, :],
                                       rhs=TR[0:64, a_r * 8 + h, :], start=True, stop=True)
                    return ins
                return f
            op('tensor', gram(0, 1, 4 * 512), reads=['C_TR'], writes=['ps4', 'ps5'])
            if getattr(self, 'cstop', 99) == 7:
                continue
            for hf in range(2):
                op('vector', lambda e, hf=hf: e.tensor_tensor(
                    out=CB[0][:, 4 * hf:4 * hf + 4, 0, :], in0=self.bank(4 + hf).rearrange("p (h f) -> p h f", h=4),
                    in1=E[1][:, 4 * hf:4 * hf + 4, :], op=ALU.mult), reads=['ps4', 'ps5', 'C_E1', 'C_CB0'], writes=['C_CB0'])
            op('tensor', gram(1, 0, 0), reads=['C_TR'], writes=['ps0', 'ps1'])
            for hf in range(2):
                op('vector', lambda e, hf=hf: e.tensor_tensor(
                    out=Pm[0][:, 4 * hf:4 * hf + 4, :], in0=self.bank(hf).rearrange("p (h f) -> p h f", h=4),
                    in1=E[2][:, 4 * hf:4 * hf + 4, :], op=ALU.mult), reads=['ps0', 'ps1', 'C_E2', 'C_P0'], writes=['C_P0'])
            op('tensor', gram(0, 2, 4 * 512), reads=['C_TR'], writes=['ps4', 'ps5'])
            for hf in range(2):
                op('vector', lambda e, hf=hf: e.tensor_tensor(
                    out=IT[:, 4 * hf:4 * hf + 4, :], in0=self.bank(4 + hf).rearrange("p (h f) -> p h f", h=4),
                    in1=E[0][:, 4 * hf:4 * hf + 4, :], op=ALU.mult), reads=['ps4', 'ps5', 'C_E0', 'C_IT'], writes=['C_IT'])
            op('vector', lambda e: e.scalar_tensor_tensor(
                out=CB[1][:, :, 1, :], in0=CB[0][:, :, 0, :], scalar=-1.0,
                in1=I8, op0=ALU.mult, op1=ALU.add),
               reads=['C_CB0', 'C_I8'], writes=['C_CB1'])
            if getattr(self, 'cstop', 99) <= 8:
                continue
            for stp in range(1, 7):
                cr, cw = CB[(stp - 1) % 2], CB[stp % 2]
                pr, pw = Pm[(stp - 1) % 2], Pm[stp % 2]
                rcr, rcw = 'C_CB%d' % ((stp - 1) % 2), 'C_CB%d' % (stp % 2)
                rpr, rpw = 'C_P%d' % ((stp - 1) % 2), 'C_P%d' % (stp % 2)
                for hf in range(2):
                    bx = 3 * hf
                    hs = range(4 * hf, 4 * hf + 4)

                    def inv(e, stp=stp, cr=cr, pr=pr, bx=bx, hs=hs):
                        ins = None
                        for hh, h in enumerate(hs):
                            if stp == 1:
                                e.matmul(ps[:, bx * 512 + hh * 256: bx * 512 + hh * 256 + 128], lhsT=pr[:, h, :],
                                         rhs=cr[:, h, 0, :], start=True, stop=True)
                            elif stp < 6:
                                e.matmul(ps[:, bx * 512 + hh * 256: bx * 512 + (hh + 1) * 256], lhsT=pr[:, h, :],
                                         rhs=cr[:, h, :, :].rearrange("p a f -> p (a f)"), start=True, stop=True)
                            else:
                                ins = e.matmul(ps[:, bx * 512 + hh * 256 + 128: bx * 512 + (hh + 1) * 256], lhsT=pr[:, h, :],
                                               rhs=cr[:, h, 1, :], start=True, stop=True)
                            if stp < 6:
                                ins = e.matmul(ps[:, (bx + 2) * 512 + hh * 128: (bx + 2) * 512 + (hh + 1) * 128],
                                               lhsT=cr[:, h, 0, :], rhs=pr[:, h, :], start=True, stop=True)
                        return ins
                    rb = ['ps%d' % (bx + q) for q in range(3)]
                    op('tensor', inv, reads=[rcr, rpr], writes=rb)
                    pxv = ps[:, bx * 512:(bx + 2) * 512].rearrange("p (h a f) -> p h a f", h=4, a=2)
                    hsl = slice(4 * hf, 4 * hf + 4)
                    if stp < 5:
                        op('scalar', lambda e, cw=cw, pxv=pxv, hsl=hsl: e.activation(out=cw[:, hsl, 0, :], in_=pxv[:, :, 0, :],
                                                                                    func=AF.Copy), reads=rb, writes=[rcw])
                    if stp < 6:
                        op('scalar', lambda e, pw=pw, bx=bx, hsl=hsl: e.activation(
                            out=pw[:, hsl, :], in_=self.bank(bx + 2).rearrange("p (h f) -> p h f", h=4), func=AF.Copy),
                           reads=rb, writes=[rpw])
                    if 2 <= stp < 6:
                        op('vector', lambda e, cw=cw, cr=cr, pxv=pxv, hsl=hsl: e.tensor_tensor(
                            out=cw[:, hsl, 1, :], in0=cr[:, hsl, 1, :], in1=pxv[:, :, 1, :], op=ALU.add),
                           reads=rb + [rcr], writes=[rcw])
                    if stp == 6:
                        op('vector', lambda e, cr=cr, pxv=pxv, hsl=hsl: e.tensor_tensor(
                            out=TTb[:, hsl, :], in0=cr[:, hsl, 1, :], in1=pxv[:, :, 1, :], op=ALU.add),
                           reads=rb + [rcr], writes=['C_TT'])
            if getattr(self, 'cstop', 99) <= 9:
                continue
            def uw(e):
                ins = None
                for h in range(8):
                    e.matmul(ps[:, 6 * 512 + h * 64: 6 * 512 + (h + 1) * 64], lhsT=TTb[:, h, :], rhs=tok[:, 3, h * 64:(h + 1) * 64],
                             start=True, stop=True)
                for h in range(8):
                    ins = e.matmul(ps[0:64, 2 * 512 + h * 128: 2 * 512 + (h + 1) * 128],
                                   lhsT=tok[:, 4, h * 64:(h + 1) * 64], rhs=TTb[:, h, :], start=True, stop=True)
                return ins
            op('tensor', uw, reads=['C_TT', 'C_tok3', 'C_tok4'], writes=['ps6', 'ps2', 'ps3'])
            op('scalar', lambda e: e.activation(out=usb, in_=self.bank(6), func=AF.Copy), reads=['ps6'], writes=['C_u'])
            for q in range(2):
                op('vector', lambda e, q=q: e.tensor_copy(out=wTb[0:64, 4 * q:4 * q + 4, :].rearrange("p m n -> p (m n)"),
                                                          in_=ps[0:64, (2 + q) * 512:(3 + q) * 512]),
                   reads=['ps2', 'ps3', 'C_wT'], writes=['C_wT'])
            if getattr(self, 'cstop', 99) <= 10:
                continue
            for cc in ((0, 1) if d == 'f' else (1, 0)):
                r0 = cc * 64
                rows = slice(r0, r0 + 64)
                cs = slice(cc * 64, cc * 64 + 64)

                def wS(e, rows=rows, cs=cs, r0=r0):
                    ins = None
                    for h in range(8):
                        ins = e.matmul(ps[rows, h * 64:(h + 1) * 64], lhsT=wTb[0:64, h, cs], rhs=Sb[0:64, h, :],
                                       start=True, stop=True, tile_position=(0, r0))
                    return ins
                op('tensor', wS, reads=['C_wT', 'C_Sb'], writes=['ps0'])
                op('vector', lambda e, rows=rows: e.tensor_tensor(out=vnb[rows].rearrange("p h d -> p (h d)"), in0=usb[rows, :],
                                                                  in1=ps[rows, 0:512], op=ALU.subtract),
                   reads=['C_u', 'ps0'], writes=['C_vn'])

                def oS(e, rows=rows, cs=cs, r0=r0):
                    ins = None
                    for h in range(8):
                        e.matmul(ps[rows, 512 + h * 64: 512 + (h + 1) * 64], lhsT=TR[0:64, 24 + h, cs], rhs=Sb[0:64, h, :],
                                 start=True, stop=True, tile_position=(0, r0))
                    for h in range(8):
                        e.matmul(ps[rows, 3 * 512 + h * 64: 3 * 512 + (h + 1) * 64], lhsT=IT[rows, h, cs], rhs=vnb[rows, h, :],
                                 start=True, stop=True, tile_position=(r0, r0))
                    for h in range(8):
                        ins = e.matmul(ps[0:64, 7 * 512 + h * 64: 7 * 512 + (h + 1) * 64],
                                       lhsT=tok[rows, 5, h * 64:(h + 1) * 64], rhs=vnb[rows, h, :], start=True, stop=True,
                                       tile_position=(r0, 0))
                    return ins
                op('tensor', oS, reads=['C_TR', 'C_Sb', 'C_IT', 'C_vn', 'C_tok5'], writes=['ps1', 'ps3', 'ps7'])
                op('vector', lambda e, cc=cc: e.tensor_tensor(out=S32[0:64], in0=S32[0:64],
                                                              in1=gtp[0:64, cc, :].unsqueeze(2).to_broadcast([64, 8, 64]),
                                                              op=ALU.mult), reads=['C_S32', 'C_gtp'], writes=['C_S32'])
                op('vector', lambda e: e.tensor_tensor(out=S32[0:64], in0=S32[0:64],
                                                       in1=ps[0:64, 7 * 512: 8 * 512].rearrange("p (m d) -> p m d", m=8),
                                                       op=ALU.add), reads=['C_S32', 'ps7'], writes=['C_S32'])
                op('scalar', lambda e: e.activation(out=Sb[0:64], in_=S32[0:64], func=AF.Copy), reads=['C_S32'], writes=['C_Sb'])
                op('scalar', lambda e, rows=rows: e.activation(out=osb[rows, :], in_=ps[rows, 512:1024], func=AF.Copy),
                   reads=['ps1'], writes=['C_osb'])
                op('vector', lambda e, rows=rows: e.tensor_tensor(out=osb[rows, :], in0=osb[rows, :], in1=ps[rows, 1536:2048],
                                                                  op=ALU.add), reads=['ps3', 'C_osb'], writes=['C_osb'])
            if d == 'f':
                dma('gpsimd', self.ofwd[t], osb, reads=['C_osb'], writes=[('ofwd', t)])
            else:
                mt, sub = t // 4, t % 4
                dma('sync', ofw, self.ofwd[t], reads=[('ofwd', t)], writes=['C_ofw'])
                op('vector', lambda e: e.tensor_tensor(out=osb, in0=osb, in1=ofw, op=ALU.add), reads=['C_osb', 'C_ofw'],
                   writes=['C_osb'])
                op('scalar', lambda e: e.activation(out=sq, in_=osb, func=AF.Square), reads=['C_osb'], writes=['C_sq'])
                op('vector', lambda e: e.tensor_reduce(out=r8, in_=sq.rearrange("p (h d) -> p h d", d=64), axis=AX.X, op=ALU.add),
                   reads=['C_sq'], writes=['C_r8'])
                op('scalar', lambda e: e.activation(out=r8, in_=r8, func=AF.Sqrt, scale=1.0 / 64, bias=RMS_EPS),
                   reads=['C_r8'], writes=['C_r8'])
                op('vector', lambda e: e.reciprocal(out=r8, in_=r8), reads=['C_r8'], writes=['C_r8'])
                ov = osb.rearrange("p (h d) -> p h d", d=64)
                op('vector', lambda e, ov=ov: e.tensor_tensor(out=ov, in0=ov, in1=r8.unsqueeze(2).to_broadcast([128, 8, 64]),
                                                              op=ALU.mult), reads=['C_osb', 'C_r8'], writes=['C_osb'])
                op('vector', lambda e, ov=ov: e.tensor_tensor(out=ov, in0=ov, in1=gn.unsqueeze(1).to_broadcast([128, 8, 64]),
                                                              op=ALU.mult), reads=['C_osb', 'C_gn'], writes=['C_osb'])
                op('vector', lambda e, tm=tm: e.tensor_tensor(out=brc, in0=osb, in1=tm[:, 1536:2048], op=ALU.mult),
                   reads=['C_osb', rtm], writes=['C_brc'])
                pb3 = self.bankb(3)

                def trc(e):
                    ins = None
                    for m in range(4):
                        ins = e.transpose(out=pb3[:, m * 128:(m + 1) * 128], in_=brc[:, m * 128:(m + 1) * 128],
                                          identity=self.identb)
                    return ins
                op('tensor', trc, reads=['C_brc', 'identb'], writes=['ps3'])
                cs_, rcs = cstg[mt % 2], 'C_cs%d' % (mt % 2)
                op('scalar', lambda e, cs_=cs_, sub=sub: e.activation(
                    out=cs_[:, :, sub * 128:(sub + 1) * 128], in_=pb3[:, 0:512].rearrange("p (m n) -> p m n", m=4),
                    func=AF.Copy), reads=['ps3'], writes=[rcs])
                if sub == 0:
                    dma('gpsimd', self.brcT[mt], cs_.rearrange("p m n -> p (m n)"), reads=[rcs], writes=[('brcT', mt)])
        self.end_stage()

    def stage_merge(self, l, slot):
        A, op, dma, ps = self.A, self.op, self.dma, self.ps
        A.push()
        WG = A.alloc((8, 3072), BF16)
        self.load_w_bf16(WG, self.w['w_gate'][l], 'WG')
        rWG = self.wres('WG', 8)
        WBa = A.alloc((8, 1024), BF16)
        sv = self.w['w_branch'][l, 0].rearrange("(h p) n -> p h n", p=64)
        for h in range(8):
            dma('gpsimd', WBa[0:64, h, :], sv[:, h, :], writes=['WBa_%d' % h])
        rWBa = self.wres('WBa', 8)
        WBb = A.alloc((4, 1024), BF16)
        WBc = A.alloc((4, 1024), BF16)
        self.load_w_bf16(WBb, self.w['w_branch'][l, 1], 'WBb')
        self.load_w_bf16(WBc, self.w['w_branch'][l, 2], 'WBc')
        WO = A.alloc((8, 1024), BF16)
        self.load_w_bf16(WO, self.w['w_out'][l], 'WO')
        rWO = self.wres('WO', 8)
        bgt = A.alloc(24, F32)
        dma('sync', bgt, self.w['b_gate'][l].rearrange("(c p) -> p c", p=128), writes=['m_bgt'],
            allow_slow_non_contiguous=True)
        self.load_ln_params(self.w['ln1_g'][l:l + 1, :], self.w['ln1_b'][l:l + 1, :])
        xm = A.alloc((8, 512), BF16)
        ba = A.alloc((8, 512), BF16)
        bb = A.alloc((4, 512), BF16)
        bc = A.alloc((4, 512), BF16)
        mT = A.alloc((8, 512), BF16)
        gs = [A.alloc(512, F32) for _ in range(3)]
        acc = A.alloc(512, F32)
        tmp = A.alloc(512, F32)
        xin = [A.alloc(D, F32) for _ in range(1)]
        xo = A.alloc(D, F32)
        xb = A.alloc(D, BF16)
        junk = A.alloc(D, F32)
        st = A.alloc(8, F32)
        stg = [A.alloc((8, 512), BF16) for _ in range(1)]
        gcnt = 0
        for mt in range(self.NM):
            dma('sync', xm, self.xT[:, :, 1 + mt * 512: 1 + (mt + 1) * 512], reads=[('xT', mt)], writes=['m_xm'])
            dma('sync', ba[0:64].rearrange("p h n -> p (h n)"), self.braT[mt], reads=[('braT', mt)], writes=['m_ba'])
            dma('sync', bb.rearrange("p h n -> p (h n)"), self.brbT[mt], reads=[('brbT', mt)], writes=['m_bb'])
            dma('sync', bc.rearrange("p h n -> p (h n)"), self.brcT[mt], reads=[('brcT', mt)], writes=['m_bc'])
            for oc in range(8):
                for b in range(3):
                    gb = gcnt % 3
                    pb = 3 + gcnt % 3
                    gcnt += 1

                    def mg(e, b=b, oc=oc, gb=gb):
                        ins = None
                        for k in range(8):
                            ins = e.matmul(self.bank(gb), lhsT=WG[:, k, b * 1024 + oc * 128: b * 1024 + (oc + 1) * 128],
                                           rhs=xm[:, k, :], start=(k == 0), stop=(k == 7))
                        return ins
                    op('tensor', mg, reads=['m_xm'] + rWG, writes=['ps%d' % gb])
                    op('scalar', lambda e, b=b, oc=oc, gb=gb: e.activation(out=gs[b], in_=self.bank(gb), func=AF.Sigmoid,
                                                                           bias=bgt[:, b * 8 + oc: b * 8 + oc + 1]),
                       reads=['ps%d' % gb, 'm_bgt'], writes=['m_gs%d' % b])

                    def mp(e, b=b, oc=oc, pb=pb):
                        ins = None
                        if b == 0:
                            for h in range(8):
                                ins = e.matmul(self.bank(pb), lhsT=WBa[0:64, h, oc * 128:(oc + 1) * 128], rhs=ba[0:64, h, :],
                                               start=(h == 0), stop=(h == 7))
                        else:
                            W_, x_ = (WBb, bb) if b == 1 else (WBc, bc)
                            for k in range(4):
                                ins = e.matmul(self.bank(pb), lhsT=W_[:, k, oc * 128:(oc + 1) * 128], rhs=x_[:, k, :],
                                               start=(k == 0), stop=(k == 3))
                        return ins
                    rsrc = [['m_ba'] + rWBa, ['m_bb'] + self.wres('WBb', 4), ['m_bc'] + self.wres('WBc', 4)][b]
                    op('tensor', mp, reads=rsrc, writes=['ps%d' % pb])
                    if b == 0:
                        op('vector', lambda e, pb=pb: e.tensor_tensor(out=acc, in0=gs[0], in1=self.bank(pb), op=ALU.mult),
                           reads=['m_gs0', 'ps%d' % pb], writes=['m_acc'])
                    elif b == 1:
                        op('vector', lambda e, pb=pb: e.tensor_tensor(out=tmp, in0=gs[1], in1=self.bank(pb), op=ALU.mult),
                           reads=['m_gs1', 'ps%d' % pb], writes=['m_tmp'])
                        op('vector', lambda e: e.tensor_tensor(out=acc, in0=acc, in1=tmp, op=ALU.add),
                           reads=['m_acc', 'm_tmp'], writes=['m_acc'])
                    else:
                        op('vector', lambda e, pb=pb: e.tensor_tensor(out=tmp, in0=gs[2], in1=self.bank(pb), op=ALU.mult),
                           reads=['m_gs2', 'ps%d' % pb], writes=['m_tmp'])
                        op('vector', lambda e, oc=oc: e.tensor_tensor(out=mT[:, oc, :], in0=acc, in1=tmp, op=ALU.add),
                           reads=['m_acc', 'm_tmp'], writes=['m_mT'])
            sg, rsg = stg[0], 'm_stg0'
            for sub in range(4):
                t = mt * 4 + sub
                xi, rxi = xin[0], 'm_xin0'
                dma('sync', xi, self.xres[t], reads=[('xres', t)], writes=[rxi])

                def my(e, sub=sub):
                    ins = None
                    for hf in range(2):
                        for k in range(8):
                            ins = e.matmul(self.bank(6 + hf), lhsT=mT[:, k, sub * 128:(sub + 1) * 128],
                                           rhs=WO[:, k, hf * 512:(hf + 1) * 512], start=(k == 0), stop=(k == 7))
                    return ins
                op('tensor', my, reads=['m_mT'] + rWO, writes=['ps6', 'ps7'])
                for hf in range(2):
                    op('vector', lambda e, xi=xi, hf=hf: e.scalar_tensor_tensor(
                        out=xi[:, hf * 512:(hf + 1) * 512], in0=xi[:, hf * 512:(hf + 1) * 512], scalar=ALPHA,
                        in1=self.bank(6 + hf), op0=ALU.mult, op1=ALU.add), reads=[rxi, 'ps%d' % (6 + hf)], writes=[rxi])
                self.ln_tile(xi, rxi, xo, 'm_xo', junk, 'm_junk', st, 'm_st')
                dma('gpsimd', self.x1res[t], xo, reads=['m_xo'], writes=[('x1res', t)])
                self.transpose_to_stage(xo, 'm_xo', xb, 'm_xb', sg, rsg, sub, 5)
            dma('gpsimd', self.x1T[mt], sg.rearrange("p k n -> p (k n)"), reads=[rsg], writes=[('x1T', mt)])
        self.end_stage()

    def stage_f1(self, l):
        A, op, dma = self.A, self.op, self.dma
        A.push()
        WF = A.alloc((8, 2 * FFN_H), BF16)
        self.load_w_bf16(WF, self.w['w_ffn_in'][l], 'WF')
        rWF = self.wres('WF', 8)
        xm = [A.alloc((8, 512), BF16) for _ in range(2)]
        ast = [A.alloc((22, 512), BF16) for _ in range(2)]
        sgt = [A.alloc(512, F32) for _ in range(2)]
        cnt = 0
        for mt in range(self.NM):
            x_, rx = xm[mt % 2], 'f_xm%d' % (mt % 2)
            dma('sync', x_.rearrange("p k n -> p (k n)"), self.x1T[mt], reads=[('x1T', mt)], writes=[rx])
            a_, ra = ast[mt % 2], 'f_ast%d' % (mt % 2)
            for cc in range(22):
                i = cnt % 2
                cnt += 1

                def mf(e, cc=cc, i=i, x_=x_):
                    ins = None
                    for k in range(8):
                        e.matmul(self.bank(i), lhsT=WF[:, k, cc * 128:(cc + 1) * 128], rhs=x_[:, k, :],
                                 start=(k == 0), stop=(k == 7))
                    for k in range(8):
                        ins = e.matmul(self.bank(2 + i), lhsT=WF[:, k, FFN_H + cc * 128: FFN_H + (cc + 1) * 128], rhs=x_[:, k, :],
                                       start=(k == 0), stop=(k == 7))
                    return ins
                op('tensor', mf, reads=[rx] + rWF, writes=['ps%d' % i, 'ps%d' % (2 + i)])
                op('scalar', lambda e, i=i: e.activation(out=sgt[i], in_=self.bank(i), func=AF.Silu),
                   reads=['ps%d' % i], writes=['f_sg%d' % i])
                op('vector', lambda e, i=i, a_=a_, cc=cc: e.tensor_tensor(out=a_[:, cc, :], in0=sgt[i], in1=self.bank(2 + i),
                                                                          op=ALU.mult),
                   reads=['f_sg%d' % i, 'ps%d' % (2 + i)], writes=[ra])
            dma('gpsimd', self.aT[mt], a_.rearrange("p c n -> p (c n)"), reads=[ra], writes=[('aT', mt)])
        self.end_stage()

    def stage_f2(self, l, slot, last):
        A, op, dma = self.A, self.op, self.dma
        A.push()
        WFO = A.alloc((22, 1024), BF16)
        svo = self.w['w_ffn_out'][l].rearrange("(c p) n -> p c n", p=128)
        for cc in range(22):
            dma('gpsimd', WFO[:, cc, :], svo[:, cc, :], writes=['WFO_%d' % cc])
        rWFO = self.wres('WFO', 22)
        WPG = A.alloc((8, 1024), BF16)
        self.load_w_bf16(WPG, self.w['w_ple_gate'][l], 'WPG')
        rWPG = self.wres('WPG', 8)
        WPP = A.alloc((2, 1024), BF16)
        self.load_w_bf16(WPP, self.w['w_ple_proj'][l], 'WPP')
        rWPP = self.wres('WPP', 2)
        bpg = A.alloc(1024, BF16)
        dma('gpsimd', bpg[0:1, :], self.w['b_ple_gate'][l:l + 1, :], writes=['g_bpg'])
        self.load_ln_params(self.w['ln2_g'][l:l + 1, :], self.w['ln2_b'][l:l + 1, :])
        am = A.alloc((22, 512), BF16)
        xm = A.alloc((8, 512), BF16)
        pb16 = [A.alloc(PLE, BF16) for _ in range(2)]
        pT = A.alloc((2, 128), BF16)
        sg = A.alloc(D, F32)
        xin = [A.alloc(D, F32) for _ in range(2)]
        xo = A.alloc(D, F32)
        xb = A.alloc(D, BF16)
        junk = A.alloc(D, F32)
        st = A.alloc(8, F32)
        stg = [A.alloc((8, 512), BF16) for _ in range(2)]
        pbv = self.bankb(6)
        for mt in range(self.NM):
            dma('sync', am.rearrange("p c n -> p (c n)"), self.aT[mt], reads=[('aT', mt)], writes=['g_am'])
            dma('sync', xm.rearrange("p k n -> p (k n)"), self.x1T[mt], reads=[('x1T', mt)], writes=['g_xm'])
            sgm, rsg = stg[mt % 2], 'g_stg%d' % (mt % 2)
            for sub in range(4):
                t = mt * 4 + sub
                i = t % 2
                xi, rxi = xin[i], 'g_xin%d' % i
                dma('sync', xi, self.x1res[t], reads=[('x1res', t)], writes=[rxi])
                dma('gpsimd', pb16[i], self.p[l, slot, t * 128:(t + 1) * 128, :], writes=['g_pb%d' % i])

                def trp(e, i=i):
                    e.transpose(out=pbv[:, 0:128], in_=pb16[i][:, 0:128], identity=self.identb)
                    return e.transpose(out=pbv[:, 128:256], in_=pb16[i][:, 128:256], identity=self.identb)
                op('tensor', trp, reads=['g_pb%d' % i, 'identb'], writes=['ps6'])
                op('scalar', lambda e: e.activation(out=pT.rearrange("p a n -> p (a n)"), in_=pbv[:, 0:256], func=AF.Copy),
                   reads=['ps6'], writes=['g_pT'])

                def mfo(e, sub=sub):
                    ins = None
                    for hf in range(2):
                        for cc in range(22):
                            ins = e.matmul(self.bank(hf), lhsT=am[:, cc, sub * 128:(sub + 1) * 128],
                                           rhs=WFO[:, cc, hf * 512:(hf + 1) * 512], start=(cc == 0), stop=(cc == 21))
                    return ins
                op('tensor', mfo, reads=['g_am'] + rWFO, writes=['ps0', 'ps1'])

                def mpg(e, sub=sub):
                    ins = None
                    for hf in range(2):
                        for k in range(8):
                            e.matmul(self.bank(2 + hf), lhsT=xm[:, k, sub * 128:(sub + 1) * 128],
                                     rhs=WPG[:, k, hf * 512:(hf + 1) * 512], start=(k == 0), stop=False)
                        ins = e.matmul(self.bank(2 + hf), lhsT=self.onesb[0:1, 0:128], rhs=bpg[0:1, hf * 512:(hf + 1) * 512],
                                       start=False, stop=True)
                    return ins
                op('tensor', mpg, reads=['g_xm', 'g_bpg', 'onesb'] + rWPG, writes=['ps2', 'ps3'])

                def mpp(e):
                    ins = None
                    for hf in range(2):
                        for k in range(2):
                            ins = e.matmul(self.bank(4 + hf), lhsT=pT[:, k, :], rhs=WPP[:, k, hf * 512:(hf + 1) * 512],
                                           start=(k == 0), stop=(k == 1))
                    return ins
                op('tensor', mpp, reads=['g_pT'] + rWPP, writes=['ps4', 'ps5'])
                for hf in range(2):
                    hs = slice(hf * 512, (hf + 1) * 512)
                    op('scalar', lambda e, hf=hf, hs=hs: e.activation(out=sg[:, hs], in_=self.bank(2 + hf), func=AF.Sigmoid),
                       reads=['ps%d' % (2 + hf), 'g_sg'], writes=['g_sg'])
                    op('vector', lambda e, xi=xi, hf=hf, hs=hs: e.scalar_tensor_tensor(
                        out=xi[:, hs], in0=xi[:, hs], scalar=ALPHA, in1=self.bank(hf), op0=ALU.mult, op1=ALU.add),
                       reads=[rxi, 'ps%d' % hf], writes=[rxi])
                    op('vector', lambda e, hf=hf, hs=hs: e.tensor_tensor(out=sg[:, hs], in0=sg[:, hs], in1=self.bank(4 + hf),
                                                                         op=ALU.mult),
                       reads=['g_sg', 'ps%d' % (4 + hf)], writes=['g_sg'])
                op('vector', lambda e, xi=xi: e.tensor_tensor(out=xi, in0=xi, in1=sg, op=ALU.add), reads=[rxi, 'g_sg'],
                   writes=[rxi])
                self.ln_tile(xi, rxi, xo, 'g_xo', junk, 'g_junk', st, 'g_st')
                if last:
                    dma('gpsimd', self.y[slot, t * 128:(t + 1) * 128, :], xo, reads=['g_xo'], writes=[('y', slot, t)])
                else:
                    dma('gpsimd', self.xres[t], xo, reads=['g_xo'], writes=[('xres', t)])
                    self.transpose_to_stage(xo, 'g_xo', xb, 'g_xb', sgm, rsg, sub, 7)
            if not last:
                dma('gpsimd', self.xT[:, :, 1 + mt * 512: 1 + (mt + 1) * 512], sgm, reads=[rsg], writes=[('xT', mt)])
        self.end_stage()

    def finalize(self):
        self.P.finish()
        self.P.emit(self.nc)
        return self.nc


def run_stages(B, slot, stages, layers=None):
    layers = range(B.L) if layers is None else layers
    if 'embed' in stages:
        B.stage_embed_ln(slot, do_ln=True)
    if 'copyin' in stages:
        B.stage_embed_ln(slot, do_ln=False)
    for l in layers:
        for st in ['p1a', 'attn', 'p1b', 'na', 'p1c', 'cf', 'cb', 'merge', 'f1', 'f2']:
            if st not in stages:
                continue
            if st == 'p1a':
                B.stage_p1a(l)
            elif st == 'attn':
                B.stage_attn()
            elif st == 'p1b':
                B.stage_p1b(l)
            elif st == 'na':
                B.stage_na(l)
            elif st == 'p1c':
                B.stage_p1c(l)
            elif st == 'cf':
                B.stage_cpass(l, 'f')
            elif st == 'cb':
                B.stage_cpass(l, 'b')
            elif st == 'merge':
                B.stage_merge(l, slot)
            elif st == 'f1':
                B.stage_f1(l)
            elif st == 'f2':
                B.stage_f2(l, slot, last=(l == B.L - 1))


ALL_STAGES = ['embed', 'p1a', 'attn', 'p1b', 'na', 'p1c', 'cf', 'cb', 'merge', 'f1', 'f2']
_CACHE = {}


def build_program(S, depth, nslot):
    key = (S, depth, nslot)
    if key not in _CACHE:
        B = Builder(S, depth, nslot)
        B.setup_consts()
        for slot in range(nslot):
            run_stages(B, slot, ALL_STAGES)
        _CACHE[key] = B.finalize()
    return _CACHE[key]


def build_layer_program(S, first):
    key = ('layer', S, first)
    if key not in _CACHE:
        B = Builder(S, 1, 1)
        B.setup_consts()
        run_stages(B, 0, (['embed'] if first else ['copyin']) + ALL_STAGES[1:])
        _CACHE[key] = B.finalize()
    return _CACHE[key]


def kernel(x_prompt, x_sample, p_prompt, p_sample, emb_ln_g, emb_ln_b, w_in,
           att_q_norm_g, att_k_norm_g, na_rpb, dn_conv_w, dn_a_log, dn_dt_bias, dn_norm_g,
           w_gate, b_gate, w_branch, w_out, ln1_g, ln1_b, w_ffn_in, w_ffn_out,
           w_ple_gate, b_ple_gate, w_ple_proj, ln2_g, ln2_b):
    f = lambda a: np.ascontiguousarray(np.asarray(a, dtype=np.float32))
    x_prompt, x_sample, p_prompt, p_sample = f(x_prompt), f(x_sample), f(p_prompt), f(p_sample)
    S = x_prompt.shape[1]
    depth = w_in.shape[0]
    per_layer = {'w_in': w_in, 'att_q_norm_g': att_q_norm_g, 'att_k_norm_g': att_k_norm_g, 'dn_conv_w': dn_conv_w,
                 'dn_a_log': dn_a_log, 'dn_dt_bias': dn_dt_bias, 'dn_norm_g': dn_norm_g, 'w_gate': w_gate,
                 'b_gate': b_gate, 'w_branch': w_branch, 'w_out': w_out, 'ln1_g': ln1_g, 'ln1_b': ln1_b,
                 'w_ffn_in': w_ffn_in, 'w_ffn_out': w_ffn_out, 'w_ple_gate': w_ple_gate, 'b_ple_gate': b_ple_gate,
                 'w_ple_proj': w_ple_proj, 'ln2_g': ln2_g, 'ln2_b': ln2_b}
    per_layer = {k: f(v) for k, v in per_layer.items()}
    rpbT = expand_rpb(f(na_rpb))
    const = {'emb_ln_g': f(emb_ln_g), 'emb_ln_b': f(emb_ln_b), 'consts': make_consts(), 'rope': make_rope(S),
             'namask': make_namask(S)}
    outs = []
    for xg, pg in ((x_prompt, p_prompt), (x_sample, p_sample)):
        n = xg.shape[0]
        cur = [xg[c] for c in range(n)]
        for l in range(depth):
            nc = build_layer_program(S, l == 0)
            shared = dict(const)
            for k, v in per_layer.items():
                shared[k] = np.ascontiguousarray(v[l:l + 1])
            shared['rpbT'] = np.ascontiguousarray(rpbT[l:l + 1])
            in_maps = []
            for c in range(n):
                m = dict(shared)
                m['x'] = np.ascontiguousarray(cur[c][None])
                m['p'] = np.ascontiguousarray(pg[l, c][None, None])
                in_maps.append(m)
            res = run_bass_kernel_spmd(nc, in_maps, core_ids=list(range(n)))
            cur = [np.asarray(r['y'])[0] for r in res.results]
        outs.append(np.stack(cur, axis=0).astype(np.float32))
    return (outs[0], outs[1])
```
